# Optimizing a Trainium2 kernel written in Bass

```python
import jax, jax.numpy as jnp
from jax import lax
import numpy as np

D_MODEL = 1024
BATCH = 8
SEQ = 2048
DEPTH = 4

CTX_LEN = 256
GRID_W = 64
HEAD_DIM = 64
D_MIX = D_MODEL
D_CONV = D_MIX // 4
D_RET = D_MIX // 4
D_FNET = D_MIX // 4
D_NAT = D_MIX // 4
N_RET_HEADS = D_RET // HEAD_DIM
N_NAT_HEADS = D_NAT // HEAD_DIM
N_FNET_GROUPS = 4
FNET_GROUP = D_FNET // N_FNET_GROUPS
CHUNK = 128
WIN_H = 8
WIN_W = 16
D_FF = 2816
ROPE_BASE = 10000.0
EPS = 1e-6
NEG = -1e30
D_IN = 3 * D_CONV + 4 * D_RET + D_FNET + 3 * D_NAT
GROUP_OFFSETS = tuple(int(v) for v in np.cumsum([0, 3 * D_CONV, 4 * D_RET, D_FNET, 3 * D_NAT]))

kernel_name = 'hybrid_diffusion_block'

f32 = jnp.float32


def rmsnorm(x, g):
    xf = x.astype(f32)
    y = xf * lax.rsqrt(jnp.mean(xf * xf, -1, keepdims=True) + EPS)
    return (y * g.astype(f32)).astype(x.dtype)


def dwconv3(u, w):
    up = jnp.pad(u, ((0, 0), (1, 1), (0, 0)))
    return up[:, :-2] * w[0] + up[:, 1:-1] * w[1] + up[:, 2:] * w[2]


def axial_rope_tables(n_tok):
    t = jnp.arange(n_tok)
    row = (t // GRID_W).astype(f32)
    col = (t % GRID_W).astype(f32)
    n_freq = HEAD_DIM // 4
    inv = ROPE_BASE ** (-jnp.arange(n_freq, dtype=f32) / n_freq)
    ang = jnp.concatenate([row[:, None] * inv, col[:, None] * inv], -1)
    return jnp.cos(ang), jnp.sin(ang)


def apply_rope(x, cos, sin):
    xf = x.astype(f32)
    x1, x2 = xf[..., :HEAD_DIM // 2], xf[..., HEAD_DIM // 2:]
    cs, sn = cos[None, :, None, :], sin[None, :, None, :]
    return jnp.concatenate([x1 * cs - x2 * sn, x1 * sn + x2 * cs], -1).astype(x.dtype)


def short_conv_mix(p, w):
    u, b, cg = jnp.split(p, 3, -1)
    return b * dwconv3(cg * u, w)


def fourier_mix(p):
    bsz, n, _ = p.shape
    pg = p.astype(f32).reshape(bsz, n, N_FNET_GROUPS, FNET_GROUP)
    f = jnp.fft.fftn(pg, axes=(1, 3), norm='ortho')
    return jnp.real(f).reshape(bsz, n, D_FNET).astype(p.dtype)


def retention_chunkwise(q, k, v, log_gamma, s0):
    bsz, n_tok, nh, hd = q.shape
    n = n_tok // CHUNK
    qc = q.astype(f32).reshape(bsz, n, CHUNK, nh, hd)
    kc = k.astype(f32).reshape(bsz, n, CHUNK, nh, hd)
    vc = v.astype(f32).reshape(bsz, n, CHUNK, nh, hd)
    pos = jnp.arange(CHUNK, dtype=f32)
    diff = pos[:, None] - pos[None, :]
    lg = log_gamma.astype(f32)
    decay = jnp.where(diff >= 0, jnp.exp(lg[:, None, None] * jnp.maximum(diff, 0.0)), 0.0)
    inner = jnp.einsum('bnihd,bnjhd->bnhij', qc, kc) * decay
    y_inner = jnp.einsum('bnhij,bnjhe->bnihe', inner, vc)
    q_decay = jnp.exp(lg[:, None] * (pos + 1.0))
    k_decay = jnp.exp(lg[:, None] * (CHUNK - 1.0 - pos))
    chunk_decay = jnp.exp(lg * CHUNK)[None, :, None, None]
    kv = jnp.einsum('bnjhd,hj,bnjhe->bnhde', kc, k_decay, vc)

    def step(s, kv_n):
        return chunk_decay * s + kv_n, s

    s_final, s_prev = lax.scan(step, s0.astype(f32), jnp.moveaxis(kv, 1, 0))
    s_prev = jnp.moveaxis(s_prev, 0, 1)
    y_cross = jnp.einsum('bnihd,hi,bnhde->bnihe', qc, q_decay, s_prev)
    return (y_inner + y_cross).reshape(bsz, n_tok, nh, hd), s_final


def gated_groupnorm(y, g, dtype):
    mu = jnp.mean(y, -1, keepdims=True)
    var = jnp.mean(jnp.square(y - mu), -1, keepdims=True)
    yn = (y - mu) * lax.rsqrt(var + EPS)
    bsz, n = y.shape[0], y.shape[1]
    return (jax.nn.silu(g.astype(f32)) * yn.reshape(bsz, n, -1)).astype(dtype)


def retention_mix(px, pc, decay_param, cos, sin):
    log_gamma = -jnp.exp(decay_param.astype(f32))

    def heads(t):
        return t.reshape(t.shape[0], t.shape[1], N_RET_HEADS, HEAD_DIM)

    qx, kx, vx, gx = jnp.split(px, 4, -1)
    qc, kc, vc, gc = jnp.split(pc, 4, -1)
    qx, kx, vx = heads(qx), heads(kx), heads(vx)
    qc, kc, vc = heads(qc), heads(kc), heads(vc)
    qx = apply_rope(qx, cos, sin)
    kx = apply_rope(kx, cos, sin) * HEAD_DIM ** -0.5
    kc = kc * HEAD_DIM ** -0.5
    s0 = jnp.zeros((px.shape[0], N_RET_HEADS, HEAD_DIM, HEAD_DIM), f32)
    fl = lambda t: jnp.flip(t, 1)
    yc_f, s_f = retention_chunkwise(qc, kc, vc, log_gamma[0], s0)
    yc_b, s_b = retention_chunkwise(fl(qc), fl(kc), fl(vc), log_gamma[1], s0)
    yx_f, _ = retention_chunkwise(qx, kx, vx, log_gamma[0], s_f)
    yx_b, _ = retention_chunkwise(fl(qx), fl(kx), fl(vx), log_gamma[1], s_b)
    out_x = gated_groupnorm(yx_f + fl(yx_b), gx, px.dtype)
    out_c = gated_groupnorm(yc_f + fl(yc_b), gc, pc.dtype)
    return out_x, out_c


def nat_mix(px, pc, rpb, rows, with_ctx):
    bsz, n_tok, _ = px.shape
    nh, hd = N_NAT_HEADS, HEAD_DIM
    qx, kx, vx = [t.reshape(bsz, n_tok, nh, hd) for t in jnp.split(px, 3, -1)]
    qc, kc, vc = [t.reshape(bsz, pc.shape[1], nh, hd) for t in jnp.split(pc, 3, -1)]
    scale = hd ** -0.5
    kh = min(WIN_H, rows)
    n_cb = GRID_W // WIN_W
    span = 2 * WIN_W
    r = jnp.arange(rows)
    row_idx = jnp.clip(r - kh // 2, 0, rows - kh)[:, None] + jnp.arange(kh)
    cb = jnp.arange(n_cb)
    col_idx = jnp.clip(cb * WIN_W - WIN_W // 2, 0, GRID_W - span)[:, None] + jnp.arange(span)
    q_col = cb[:, None] * WIN_W + jnp.arange(WIN_W)
    q_col_start = jnp.clip(q_col - WIN_W // 2, 0, GRID_W - WIN_W)
    kcol = col_idx[:, None, :]
    in_win = (kcol >= q_col_start[..., None]) & (kcol < q_col_start[..., None] + WIN_W)
    drow = row_idx - r[:, None]
    dcol = jnp.clip(kcol - q_col[:, :, None] + WIN_W - 1, 0, 2 * WIN_W - 2)
    bias = rpb.astype(f32)[:, (drow + WIN_H - 1)[:, None, None, :, None], dcol[None, :, :, None, :]]

    gr = row_idx[:, None, :, None]
    gc = col_idx[None, :, None, :]
    kg = kx.reshape(bsz, rows, GRID_W, nh, hd)[:, gr, gc]
    vg = vx.reshape(bsz, rows, GRID_W, nh, hd)[:, gr, gc]
    qg = (qx * scale).reshape(bsz, rows, n_cb, WIN_W, nh, hd)
    s_loc = jnp.einsum('brcqhd,brckshd->bhrcqks', qg, kg).astype(f32) + bias
    s_loc = jnp.where(in_win[:, :, None, :], s_loc, NEG)
    s_ctx = jnp.einsum('brcqhd,bjhd->bhrcqj', qg, kc).astype(f32)
    n_loc = kh * span
    s_all = jnp.concatenate([s_loc.reshape(s_loc.shape[:5] + (n_loc,)), s_ctx], -1)
    p = jax.nn.softmax(s_all, -1).astype(px.dtype)
    p_loc = p[..., :n_loc].reshape(s_loc.shape)
    o = (jnp.einsum('bhrcqks,brckshe->brcqhe', p_loc, vg)
         + jnp.einsum('bhrcqj,bjhe->brcqhe', p[..., n_loc:], vc))
    y_x = o.reshape(bsz, n_tok, nh * hd)
    if not with_ctx:
        return y_x, None
    s_cc = jnp.einsum('bihd,bjhd->bhij', qc * scale, kc).astype(f32)
    p_cc = jax.nn.softmax(s_cc, -1).astype(pc.dtype)
    y_c = jnp.einsum('bhij,bjhe->bihe', p_cc, vc).reshape(bsz, pc.shape[1], nh * hd)
    return y_x, y_c


def hybrid_mix(px, pc, conv_w, ret_decay, nat_rpb, rows, cos, sin, with_ctx):
    o = GROUP_OFFSETS
    sl = lambda p, i: p[..., o[i]:o[i + 1]]
    ya_x = short_conv_mix(sl(px, 0), conv_w)
    yb_x, yb_c = retention_mix(sl(px, 1), sl(pc, 1), ret_decay, cos, sin)
    yc_x = fourier_mix(sl(px, 2))
    yd_x, yd_c = nat_mix(sl(px, 3), sl(pc, 3), nat_rpb, rows, with_ctx)
    y_x = jnp.concatenate([ya_x, yb_x, yc_x, yd_x], -1)
    if not with_ctx:
        return y_x, None
    y_c = jnp.concatenate([short_conv_mix(sl(pc, 0), conv_w), yb_c, fourier_mix(sl(pc, 2)), yd_c], -1)
    return y_x, y_c


def conv_ffn(h, w_up, w_conv, w_down):
    u = dwconv3(h @ w_up, w_conv)
    a, b = jnp.split(u, 2, -1)
    return (jax.nn.silu(a) * b) @ w_down


def setup_inputs(seed: int = 0) -> dict:
    key = jax.random.key(seed)
    ks = jax.random.split(key, 20)
    D = D_MODEL
    nrm = lambda k, shape, s: jax.random.normal(k, shape, f32) * s
    gain = lambda k: 1.0 + nrm(k, (DEPTH, D), 0.02)
    ret_base = jnp.log(-jnp.log1p(-(2.0 ** (-5.0 - jnp.arange(N_RET_HEADS, dtype=f32)))))
    return {
        'x': nrm(ks[0], (BATCH, SEQ, D), 1.0),
        'c': nrm(ks[1], (BATCH, D), 1.0),
        'ctx': nrm(ks[2], (BATCH, CTX_LEN, D), 1.0),
        'c_ctx': nrm(ks[3], (D,), 1.0),
        'w_mod': nrm(ks[4], (DEPTH, D, 6 * D), 0.3 * D ** -0.5),
        'b_mod': nrm(ks[5], (DEPTH, 6 * D), 0.02),
        'g_pre_mix': gain(ks[6]),
        'g_post_mix': gain(ks[7]),
        'g_pre_ffn': gain(ks[8]),
        'g_post_ffn': gain(ks[9]),
        'w_in': nrm(ks[10], (DEPTH, D, D_IN), D ** -0.5),
        'w_out': nrm(ks[11], (DEPTH, D_MIX, D), D_MIX ** -0.5),
        'conv_w': nrm(ks[12], (DEPTH, 3, D_CONV), 3 ** -0.5),
        'ret_decay': ret_base[None, None, :] + nrm(ks[13], (DEPTH, 2, N_RET_HEADS), 0.1),
        'nat_rpb': nrm(ks[14], (DEPTH, N_NAT_HEADS, 2 * WIN_H - 1, 2 * WIN_W - 1), 0.1),
        'w_up': nrm(ks[15], (DEPTH, D, 2 * D_FF), D ** -0.5),
        'ffn_conv_w': nrm(ks[16], (DEPTH, 3, 2 * D_FF), 3 ** -0.5),
        'w_down': nrm(ks[17], (DEPTH, D_FF, D), D_FF ** -0.5),
    }


def reference(x, c, ctx, c_ctx, w_mod, b_mod, g_pre_mix, g_post_mix, g_pre_ffn, g_post_ffn,
              w_in, w_out, conv_w, ret_decay, nat_rpb, w_up, ffn_conv_w, w_down):
    n_lat = x.shape[1]
    rows = n_lat // GRID_W
    cos, sin = axial_rope_tables(n_lat)
    sc_x = jax.nn.silu(c)
    sc_c = jax.nn.silu(c_ctx)
    for l in range(DEPTH):
        with_ctx = l < DEPTH - 1
        mod_x = (sc_x @ w_mod[l] + b_mod[l])[:, None, :]
        mod_c = sc_c @ w_mod[l] + b_mod[l]
        sh1_x, s1_x, g1_x, sh2_x, s2_x, g2_x = jnp.split(mod_x, 6, -1)
        sh1_c, s1_c, g1_c, sh2_c, s2_c, g2_c = jnp.split(mod_c, 6, -1)
        hx = rmsnorm(x, g_pre_mix[l]) * (1 + s1_x) + sh1_x
        hc = rmsnorm(ctx, g_pre_mix[l]) * (1 + s1_c) + sh1_c
        px = hx @ w_in[l]
        pc = hc @ w_in[l]
        yx, yc = hybrid_mix(px, pc, conv_w[l], ret_decay[l], nat_rpb[l], rows, cos, sin, with_ctx)
        x = x + g1_x * rmsnorm(yx @ w_out[l], g_post_mix[l])
        hx = rmsnorm(x, g_pre_ffn[l]) * (1 + s2_x) + sh2_x
        x = x + g2_x * rmsnorm(conv_ffn(hx, w_up[l], ffn_conv_w[l], w_down[l]), g_post_ffn[l])
        if with_ctx:
            ctx = ctx + g1_c * rmsnorm(yc @ w_out[l], g_post_mix[l])
            hc = rmsnorm(ctx, g_pre_ffn[l]) * (1 + s2_c) + sh2_c
            ctx = ctx + g2_c * rmsnorm(conv_ffn(hc, w_up[l], ffn_conv_w[l], w_down[l]), g_post_ffn[l])
    return x
```

```python
import contextlib
import numpy as np
import ml_dtypes
import concourse.bass as bass
import concourse.mybir as mybir
from concourse.bass_utils import run_bass_kernel_spmd

F32 = mybir.dt.float32
BF16 = mybir.dt.bfloat16
AF = mybir.ActivationFunctionType
ALU = mybir.AluOpType
AX = mybir.AxisListType

D = 1024
NT = 2304
NCTX = 256
NX = 2048
DEPTH = 4
DFF = 2816
EPS = 1e-6
NWIN = 26 * 128
BLOCKS = [(0, 256), (256, 512), (768, 512), (1280, 512), (1792, 512)]

SP_GPM, SP_GPO, SP_GPF, SP_GPOF, SP_BMOD, SP_CW, SP_FCW, SP_RDB, SP_RDP, SP_N = 0, 8, 16, 24, 32, 80, 86, 218, 226, 230
CF_DF, CF_DB, CF_IL1, CF_ILB, CF_JL, CF_EPS, CF_N = 0, 128, 256, 384, 512, 514, 516
CB_ID, CB_ONE, CB_BD, CB_CS2, CB_N = 0, 128, 256, 384, 640


class Sched:
    ENGS = ("pe", "act", "dve", "pool", "sp")

    def __init__(self, nslots_hw=32, nslots_sw=24):
        self.q = {e: [] for e in self.ENGS}
        self.cnt = {e: 0 for e in self.ENGS}
        self.seen = {e: {} for e in self.ENGS}
        self.recs = {}
        self.nslots_hw, self.nslots_sw = nslots_hw, nslots_sw
        self.nslots = nslots_hw + nslots_sw
        self.slot_cnt = [0] * self.nslots
        self.next_hw = 0
        self.next_sw = 0
        self.nops = 0

    def add(self, eng, fn, reads=(), writes=(), dma=False):
        writes = list(writes) + [t for t in reads if t[0] == "ps"]
        deps = {}
        for (space, lo, hi) in reads:
            for rec in self.recs.setdefault(space, []):
                if rec[0] < hi and lo < rec[1] and rec[2] is not None:
                    s, v = rec[2]
                    if deps.get(s, 0) < v:
                        deps[s] = v
        for (space, lo, hi) in writes:
            for rec in self.recs.setdefault(space, []):
                if rec[0] < hi and lo < rec[1]:
                    if rec[2] is not None:
                        s, v = rec[2]
                        if deps.get(s, 0) < v:
                            deps[s] = v
                    for s, v in rec[3].items():
                        if deps.get(s, 0) < v:
                            deps[s] = v
        if dma:
            if eng == "pool":
                sl = self.nslots_hw + self.next_sw
                self.next_sw = (self.next_sw + 1) % self.nslots_sw
            else:
                sl = self.next_hw
                self.next_hw = (self.next_hw + 1) % self.nslots_hw
            sem = ("dma", sl)
            if self.slot_cnt[sl] > 0:
                v = 16 * self.slot_cnt[sl]
                if deps.get(sem, 0) < v:
                    deps[sem] = v
            self.slot_cnt[sl] += 1
            handle = (sem, 16 * self.slot_cnt[sl])
        else:
            self.cnt[eng] += 1
            handle = (("eng", eng), self.cnt[eng])
        waits = []
        seen = self.seen[eng]
        for sem, val in deps.items():
            if sem == ("eng", "pe") and eng == "pe":
                continue
            if seen.get(sem, 0) >= val:
                continue
            seen[sem] = val
            waits.append((sem, val))
        self.q[eng].append((waits, fn, handle))
        self.nops += 1
        for (space, lo, hi) in reads:
            recs = self.recs[space]
            for rec in recs:
                if rec[0] == lo and rec[1] == hi:
                    if rec[3].get(handle[0], 0) < handle[1]:
                        rec[3][handle[0]] = handle[1]
                    break
            else:
                recs.append([lo, hi, None, {handle[0]: handle[1]}])
        for (space, lo, hi) in writes:
            recs = self.recs[space]
            recs[:] = [r for r in recs if not (lo <= r[0] and r[1] <= hi)]
            recs.append([lo, hi, handle, {}])
        return handle

    def emit(self, nc, final_waits):
        with contextlib.ExitStack() as es:
            sems = {}
            for e in self.ENGS:
                sems[("eng", e)] = es.enter_context(nc.semaphore("s_" + e))
            for s in range(self.nslots):
                sems[("dma", s)] = es.enter_context(nc.semaphore("d%d" % s))
            block = es.enter_context(nc.Block())

            def run(engname, engobj):
                for (waits, fn, handle) in self.q[engname]:
                    for (sem, val) in waits:
                        engobj.wait_ge(sems[sem], val)
                    inst = fn(engobj)
                    inst.then_inc(sems[handle[0]], 16 if handle[0][0] == "dma" else 1)
                if engname == "sp":
                    for (sem, val) in final_waits:
                        engobj.wait_ge(sems[sem], val)

            @block.tensor
            def _(e):
                run("pe", e)

            @block.scalar
            def _(e):
                run("act", e)

            @block.vector
            def _(e):
                run("dve", e)

            @block.gpsimd
            def _(e):
                run("pool", e)

            @block.sync
            def _(e):
                run("sp", e)


class V:
    __slots__ = ("ap", "tok")

    def __init__(self, ap, tok):
        self.ap = ap
        self.tok = tok

    def p(self, lo, hi):
        return V(self.ap[lo:hi], self.tok)


class Tile:
    def __init__(self, arena_ap, off, A, B, dt, space="sb"):
        self.es = 4 if dt == F32 else 2
        self.A, self.B, self.dt, self.off, self.space = A, B, dt, off, space
        nbytes = A * B * self.es
        assert off % 4 == 0
        ap = arena_ap[:, off // 4:(off + nbytes + 3) // 4]
        if dt != F32:
            ap = ap.bitcast(dt)[:, 0:A * B]
        self.flat = ap
        self.ap3 = ap.rearrange("p (a b) -> p a b", a=A) if A > 1 else None
        self.nbytes = nbytes

    def _tok(self, a, lo, hi):
        o = self.off + (a * self.B + lo) * self.es
        return (self.space, o, self.off + (a * self.B + hi) * self.es)

    def s(self, a, lo=0, hi=None):
        if hi is None:
            hi = self.B
        if self.A == 1:
            return V(self.flat[:, lo:hi], [self._tok(0, lo, hi)])
        if a is None:
            return V(self.ap3[:, :, lo:hi], [self._tok(i, lo, hi) for i in range(self.A)])
        if isinstance(a, tuple):
            return V(self.ap3[:, a[0]:a[1], lo:hi], [self._tok(i, lo, hi) for i in range(a[0], a[1])])
        return V(self.ap3[:, a, lo:hi], [self._tok(a, lo, hi)])

    def all(self):
        if self.A == 1:
            return V(self.flat, [(self.space, self.off, self.off + self.nbytes)])
        return V(self.ap3, [(self.space, self.off, self.off + self.nbytes)])


class Arena:
    def __init__(self, ap, nbytes):
        self.ap, self.nbytes, self.top = ap, nbytes, 0
        self.peak = 0

    def alloc(self, A, B, dt):
        es = 4 if dt == F32 else 2
        nb = (A * B * es + 63) // 64 * 64
        off = self.top
        self.top += nb
        self.peak = max(self.peak, self.top)
        assert self.top <= self.nbytes, "SBUF arena overflow: %d > %d" % (self.top, self.nbytes)
        return Tile(self.ap, off, A, B, dt)

    def mark(self):
        return self.top

    def release(self, m):
        self.top = m


def nat_geom(qt):
    s0 = min(max(2 * qt - 4, 0), 24)
    kr0e = min(s0 - s0 % 2, 22)
    var = {0: 1, 1: 2, 14: 3, 15: 4}.get(qt, 0)
    return kr0e, var


class _Stop(Exception):
    pass


def build_program(nlayers=DEPTH, dbg=None, stop=None):
    nc = bass.Bass("TRN2", target_bir_lowering=False)

    def chk(name):
        if stop == name:
            raise _Stop()
    S = Sched()
    dram = {}

    def din(name, shape, dt):
        dram[name] = nc.dram_tensor(name, shape, dt, kind="ExternalInput").ap()
        return dram[name]

    xin = din("xin", [128, 8, NT], F32)
    cvec = din("cvec", [128, 16], F32)
    cb_d = din("cb", [128, CB_N], BF16)
    cb2_d = din("cb2", [128, 1024], BF16)
    cf_d = din("cf", [128, CF_N], F32)
    rope_d = din("rope", [128, 2, NT], F32)
    spar_d = din("spar", [128, DEPTH, SP_N], F32)
    wmod_d = din("wmod", [DEPTH, 128, 8, 6144], F32)
    win_d = din("win", [DEPTH, 128, 8, NWIN], F32)
    wout_d = din("wout", [DEPTH, 128, 8, 1024], F32)
    wup_d = din("wup", [DEPTH, 22, 128, 8, 256], F32)
    wdn_d = din("wdn", [DEPTH, 128, 22, 1024], F32)
    natb_d = din("natb", [DEPTH, 5, 4, 128, 640], F32)
    dftc_d = din("dftc", [8, 128, 16, 256], BF16)
    dfts_d = din("dfts", [8, 128, 16, 256], BF16)
    yout = nc.dram_tensor("yout", [128, 8, NX], F32, kind="ExternalOutput").ap()
    ykind = "ExternalOutput" if dbg else "Internal"
    yd = nc.dram_tensor("yd", [8, 128, NT], BF16, kind=ykind).ap()
    if dbg:
        xdbg = nc.dram_tensor("xdbg", [128, 8, NT], F32, kind="ExternalOutput").ap()

    def ydtok(c, lo, hi):
        return [("yd", (c * NT + lo) * 2, (c * NT + hi) * 2)]

    es = contextlib.ExitStack()
    with es:
        NB = 212000
        arena_t = es.enter_context(nc.sbuf_tensor("arena", [128, NB // 4], F32))
        AR = Arena(arena_t[:], NB)
        ps_t = es.enter_context(nc.psum_tensor("psum", [128, 4096], F32))
        PS32 = ps_t[:]
        PS16 = ps_t[:].bitcast(BF16)
        bank_ctr = [0]

        def bank():
            b = bank_ctr[0]
            bank_ctr[0] = (b + 1) % 8
            return b

        def bank2():
            if bank_ctr[0] % 2:
                bank_ctr[0] = (bank_ctr[0] + 1) % 8
            b = bank_ctr[0]
            bank_ctr[0] = (b + 2) % 8
            return b

        def PB(b, lo, hi, nb=1):
            return V(PS32[:, b * 512 + lo:b * 512 + hi], [("ps", (b + i) * 2048, (b + i + 1) * 2048) for i in range(nb)])

        def PB3(b, nb, lo, hi):
            ap = PS32[:, b * 512:(b + nb) * 512].rearrange("p (a c) -> p a c", a=nb)[:, :, lo:hi]
            return V(ap, [("ps", (b + i) * 2048, (b + i + 1) * 2048) for i in range(nb)])

        def PBh(b, lo, hi):
            return V(PS16[:, b * 1024 + lo:b * 1024 + hi], [("ps", b * 2048, (b + 1) * 2048)])

        def _ap(x):
            return x.ap if isinstance(x, V) else x

        def _tk(*xs):
            t = []
            for x in xs:
                if isinstance(x, V):
                    t += x.tok
            return t

        def ACT(out, in_, func, bias=None, scale=1.0, accum=None, eng="act"):
            kw = {"scale": _ap(scale)}
            if bias is not None:
                kw["bias"] = _ap(bias)
            if accum is not None:
                kw["accum_out"] = accum.ap
            o, i = out.ap, in_.ap
            S.add(eng, lambda e: e.activation(out=o, in_=i, func=func, **kw),
                  _tk(in_, bias, scale), _tk(out, accum))

        def TT(out, in0, in1, op, eng="dve"):
            o, a, b = out.ap, in0.ap, in1.ap
            S.add(eng, lambda e: e.tensor_tensor(out=o, in0=a, in1=b, op=op), _tk(in0, in1), _tk(out))

        def TS(out, in0, s1, op0, s2=None, op1=None, eng="dve"):
            o, a = out.ap, in0.ap
            a1, a2 = _ap(s1), _ap(s2)
            if op1 is None:
                S.add(eng, lambda e: e.tensor_scalar(out=o, in0=a, scalar1=a1, scalar2=None, op0=op0), _tk(in0, s1), _tk(out))
            else:
                S.add(eng, lambda e: e.tensor_scalar(out=o, in0=a, scalar1=a1, scalar2=a2, op0=op0, op1=op1), _tk(in0, s1, s2), _tk(out))

        def STT(out, in0, scalar, in1, op0, op1, eng="dve"):
            o, a, b, sc = out.ap, in0.ap, in1.ap, _ap(scalar)
            S.add(eng, lambda e: e.scalar_tensor_tensor(out=o, in0=a, scalar=sc, in1=b, op0=op0, op1=op1),
                  _tk(in0, in1, scalar), _tk(out))

        def COPY(out, in_, eng="dve"):
            o, a = out.ap, in_.ap
            S.add(eng, lambda e: e.tensor_copy(out=o, in_=a), _tk(in_), _tk(out))

        def RECIP(out, in_):
            o, a = out.ap, in_.ap
            S.add("dve", lambda e: e.reciprocal(out=o, in_=a), _tk(in_), _tk(out))

        def MEMSET(out, val, eng="dve"):
            o = out.ap
            S.add(eng, lambda e: e.memset(o, val), [], _tk(out))

        def REDMAXNEG(out, in_):
            o, a = out.ap, in_.ap
            S.add("dve", lambda e: e.tensor_reduce(out=o, in_=a, axis=AX.X, op=ALU.max, negate=True), _tk(in_), _tk(out))

        def MM(out, terms):
            o = out.ap
            tl = [(l.ap, r.ap) for (l, r) in terms]
            rd = []
            for (l, r) in terms:
                rd += l.tok + r.tok

            def f(e):
                n = len(tl)
                for i, (l, r) in enumerate(tl):
                    ins = e.matmul(o, l, r, start=(i == 0), stop=(i == n - 1))
                return ins
            S.add("pe", f, rd, _tk(out))

        def MMG(groups):
            rd, wr, gl = [], [], []
            for g_ in groups:
                out, terms = g_[0], g_[1]
                st_f = g_[2] if len(g_) > 2 else True
                sp_f = g_[3] if len(g_) > 3 else True
                wr += out.tok
                tl = []
                for (l, r) in terms:
                    rd += l.tok + r.tok
                    tl.append((l.ap, r.ap))
                gl.append((out.ap, tl, st_f, sp_f))

            def f(e):
                for (o, tl, st_f, sp_f) in gl:
                    n = len(tl)
                    for i, (l, r) in enumerate(tl):
                        ins = e.matmul(o, l, r, start=(i == 0 and st_f), stop=(i == n - 1 and sp_f))
                return ins
            S.add("pe", f, rd, wr)

        def TRS(pairs, ident):
            rd, wr, pl = list(ident.tok), [], []
            for (o, i) in pairs:
                rd += i.tok
                wr += o.tok
                pl.append((o.ap, i.ap))
            ia = ident.ap

            def f(e):
                for (o, i) in pl:
                    ins = e.transpose(o, i, ia)
                return ins
            S.add("pe", f, rd, wr)

        def DMA(out_ap, in_ap, rd, wr, eng="sp"):
            return S.add(eng, lambda e: e.dma_start(out=out_ap, in_=in_ap), rd, wr, dma=True)

        XT = AR.alloc(8, NT, F32)
        CB = AR.alloc(1, CB_N, BF16)
        CB2 = AR.alloc(1, 1024, BF16)
        CF = AR.alloc(1, CF_N, F32)
        SPAR = AR.alloc(DEPTH, SP_N, F32)
        CVEC = AR.alloc(1, 16, F32)
        SCB = AR.alloc(8, 2, BF16)
        MOD = AR.alloc(DEPTH, 96, F32)
        DER = AR.alloc(2, 48, F32)
        LG = AR.alloc(1, 16, F32)
        DD = AR.alloc(4, 128, F32)
        GQ = AR.alloc(4, 128, F32)
        VD = AR.alloc(1, 8, F32)
        HALO = AR.alloc(8, 1, BF16)
        SMALL = AR.alloc(1, 64, F32)
        ident = CB.s(0, CB_ID, CB_ID + 128)
        ones = CB.s(0, CB_ONE, CB_ONE + 128)
        bd64 = CB.s(0, CB_BD, CB_BD + 128)
        cs2 = CB.s(0, CB_CS2, CB_CS2 + 256)
        epsc = CF.s(0, CF_EPS, CF_EPS + 1)

        def load_full(tile, src):
            v = tile.all()
            DMA(v.ap, src, [], v.tok)

        for kc in range(8):
            v = XT.s(kc)
            DMA(v.ap, xin[:, kc, :], [], v.tok)
        DMA(CB.all().ap, cb_d, [], CB.all().tok)
        DMA(CB2.all().ap, cb2_d, [], CB2.all().tok)
        DMA(CF.all().ap, cf_d, [], CF.all().tok)
        DMA(SPAR.all().ap, spar_d, [], SPAR.all().tok)
        DMA(CVEC.all().ap, cvec, [], CVEC.all().tok)
        for which in range(2):
            o = V(SCB.ap3[:, :, which], SCB.all().tok)
            ACT(o, CVEC.s(0, which * 8, which * 8 + 8), AF.Silu)

        PH = AR.mark()

        wm_bufs = [AR.alloc(8, 1536, BF16) for _ in range(2)]
        it = 0
        for l in range(1):
            for q in range(4):
                wm = wm_bufs[it % 2]
                it += 1
                DMA(wm.all().ap, wmod_d[l, :, :, q * 1536:(q + 1) * 1536], [], wm.all().tok, eng="pool")
                b = bank()
                groups = []
                for ch in range(12):
                    terms = [(wm.s(kc, ch * 128, ch * 128 + 128), SCB.s(kc)) for kc in range(8)]
                    groups.append((PB(b, ch * 2, ch * 2 + 2), terms))
                MMG(groups)
                o = V(MOD.ap3[:, l, q * 24:(q + 1) * 24].rearrange("p (c w) -> p c w", w=2), MOD.s(l, q * 24, q * 24 + 24).tok)
                i0 = V(PS32[:, b * 512:b * 512 + 24].rearrange("p (c w) -> p c w", w=2), PB(b, 0, 24).tok)
                bm = SPAR.s(l, SP_BMOD + q * 12, SP_BMOD + q * 12 + 12)
                i1 = V(bm.ap.unsqueeze(2).to_broadcast([128, 12, 2]), bm.tok)
                TT(o, i0, i1, ALU.add)
        AR.release(PH)

        def rmsnorm_to(dst_fn, t0, n, which, kgs, ksh, sq, rs, t1s):
            ACT(sq.s((0, 4), 0, n), XT.s((0, 4), t0, t0 + n), AF.Square)
            TT(sq.s((4, 8), 0, n), XT.s((4, 8), t0, t0 + n), XT.s((4, 8), t0, t0 + n), ALU.mult)
            b = bank()
            MM(PB(b, 0, n), [(ones, sq.s(kc, 0, n)) for kc in range(8)])
            ACT(rs.s(0, 0, n), PB(b, 0, n), AF.Sqrt, bias=epsc, scale=1.0 / D)
            RECIP(rs.s(0, 0, n), rs.s(0, 0, n))
            for kc in range(8):
                t1 = t1s[kc % 2]
                STT(t1.s(0, 0, n), XT.s(kc, t0, t0 + n), DER.s(which, kgs * 8 + kc, kgs * 8 + kc + 1), rs.s(0, 0, n), ALU.mult, ALU.mult)
                ACT(dst_fn(kc), t1.s(0, 0, n), AF.Identity, bias=DER.s(which, ksh * 8 + kc, ksh * 8 + kc + 1))

        def resid_update(ob, t0, n, which, kgg, sq, rs):
            ACT(sq.s((0, 4), 0, n), ob.s((0, 4), 0, n), AF.Square)
            TT(sq.s((4, 8), 0, n), ob.s((4, 8), 0, n), ob.s((4, 8), 0, n), ALU.mult)
            b = bank()
            MM(PB(b, 0, n), [(ones, sq.s(kc, 0, n)) for kc in range(8)])
            ACT(rs.s(0, 0, n), PB(b, 0, n), AF.Sqrt, bias=epsc, scale=1.0 / D)
            RECIP(rs.s(0, 0, n), rs.s(0, 0, n))
            for kc in range(8):
                TT(ob.s(kc, 0, n), ob.s(kc, 0, n), rs.s(0, 0, n), ALU.mult, eng="pool" if kc % 2 else "dve")
                STT(XT.s(kc, t0, t0 + n), ob.s(kc, 0, n), DER.s(which, kgg * 8 + kc, kgg * 8 + kc + 1), XT.s(kc, t0, t0 + n), ALU.mult, ALU.add)

        try:
          for l in range(nlayers):
              chk("prologue")
              for which in range(2):
                  def modv(k):
                      ap = MOD.ap3[:, l, k * 16:(k + 1) * 16].rearrange("p (c w) -> p c w", w=2)[:, :, which]
                      return V(ap, MOD.s(l, k * 16, k * 16 + 16).tok)
                  STT(DER.s(which, 0, 8), modv(1), 1.0, SPAR.s(l, SP_GPM, SP_GPM + 8), ALU.add, ALU.mult)
                  COPY(DER.s(which, 8, 16), modv(0))
                  TT(DER.s(which, 16, 24), modv(2), SPAR.s(l, SP_GPO, SP_GPO + 8), ALU.mult)
                  STT(DER.s(which, 24, 32), modv(4), 1.0, SPAR.s(l, SP_GPF, SP_GPF + 8), ALU.add, ALU.mult)
                  COPY(DER.s(which, 32, 40), modv(3))
                  TT(DER.s(which, 40, 48), modv(5), SPAR.s(l, SP_GPOF, SP_GPOF + 8), ALU.mult)
              ACT(LG.s(0, 0, 8), SPAR.s(l, SP_RDB, SP_RDB + 8), AF.Exp)
              TS(LG.s(0, 0, 8), LG.s(0, 0, 8), -1.0, ALU.mult)
              ACT(LG.s(0, 8, 12), SPAR.s(l, SP_RDP, SP_RDP + 4), AF.Exp)
              TS(LG.s(0, 8, 12), LG.s(0, 8, 12), -1.0, ALU.mult)
              ACT(LG.s(0, 12, 16), LG.s(0, 8, 12), AF.Exp, scale=128.0)
              for h in range(4):
                  ACT(DD.s(h), CF.s(0, CF_DF, CF_DF + 128), AF.Exp, scale=LG.s(0, h, h + 1))
                  ACT(GQ.s(0), CF.s(0, CF_DB, CF_DB + 128), AF.Exp, scale=LG.s(0, 4 + h, 5 + h))
                  TT(DD.s(h), DD.s(h), GQ.s(0), ALU.add)
              for hp in range(2):
                  ACT(GQ.s(hp * 2 + 0), CF.s(0, CF_IL1, CF_IL1 + 128), AF.Exp, scale=LG.s(0, 8 + hp * 2, 9 + hp * 2))
                  ACT(GQ.s(hp * 2 + 1), CF.s(0, CF_ILB, CF_ILB + 128), AF.Exp, scale=LG.s(0, 9 + hp * 2, 10 + hp * 2))
              for dr in range(2):
                  for h in range(4):
                      ACT(VD.s(0, dr * 4 + h, dr * 4 + h + 1), CF.s(0, CF_JL + dr, CF_JL + dr + 1), AF.Exp, scale=LG.s(0, dr * 4 + h, dr * 4 + h + 1))

              M0 = AR.mark()
              HX = AR.alloc(8, NT, BF16)
              WB = [AR.alloc(8, 1024, BF16) for _ in range(2)]
              wb_ctr = [0]

              def load_win(c0, ncols):
                  w = WB[wb_ctr[0] % 2]
                  wb_ctr[0] += 1
                  v = w.s(None, 0, ncols)
                  DMA(v.ap, win_d[l, :, :, c0:c0 + ncols], [], v.tok, eng="pool")
                  return w

              G0 = AR.mark()
              sq = AR.alloc(8, 512, BF16)
              rs = AR.alloc(1, 512, F32)
              t1s = [AR.alloc(1, 512, F32) for _ in range(2)]
              wA = load_win(0, 768)

              def norm_blk(bi_):
                  t0_, n_ = BLOCKS[bi_]
                  rmsnorm_to(lambda kc: HX.s(kc, t0_, t0_ + n_), t0_, n_, 1 if bi_ == 0 else 0, 0, 1, sq, rs, t1s)
              norm_blk(0)
              norm_blk(1)

              def proj(w, wc0, t0, n):
                  b = bank()
                  MM(PB(b, 0, n), [(w.s(kc, wc0, wc0 + 128), HX.s(kc, t0, t0 + n)) for kc in range(8)])
                  return b

              chk("norm")
              wB0 = load_win(768, 768)
              Tt = AR.alloc(1, NT, F32)
              Bt = AR.alloc(1, NT, F32)
              acc = AR.alloc(1, NT, F32)
              cgs = [AR.alloc(1, 512, F32) for _ in range(2)]
              ya = AR.alloc(1, NT, BF16)
              for j in range(2):
                  for bi, (t0, n) in enumerate(BLOCKS):
                      if j == 0 and bi + 2 < len(BLOCKS):
                          norm_blk(bi + 2)
                      bu = proj(wA, j * 384 + 0, t0, n)
                      bb = proj(wA, j * 384 + 128, t0, n)
                      bc = proj(wA, j * 384 + 256, t0, n)
                      cg = cgs[bi % 2]
                      ACT(cg.s(0, 0, n), PB(bc, 0, n), AF.Copy)
                      TT(Tt.s(0, t0, t0 + n), PB(bu, 0, n), cg.s(0, 0, n), ALU.mult)
                      ACT(Bt.s(0, t0, t0 + n), PB(bb, 0, n), AF.Copy)
                  cw = lambda k: SPAR.s(l, SP_CW + j * 3 + k, SP_CW + j * 3 + k + 1)
                  for (s0, s1) in ((0, NCTX), (NCTX, NT)):
                      TS(acc.s(0, s0, s1), Tt.s(0, s0, s1), cw(1), ALU.mult, eng="pool")
                      STT(acc.s(0, s0 + 1, s1), Tt.s(0, s0, s1 - 1), cw(0), acc.s(0, s0 + 1, s1), ALU.mult, ALU.add)
                      STT(acc.s(0, s0, s1 - 1), Tt.s(0, s0 + 1, s1), cw(2), acc.s(0, s0, s1 - 1), ALU.mult, ALU.add)
                      TT(ya.s(0, s0, s1), acc.s(0, s0, s1), Bt.s(0, s0, s1), ALU.mult, eng="pool")
                  DMA(yd[j], ya.all().ap, ya.all().tok, ydtok(j, 0, NT))
              AR.release(G0)

              chk("A")
              wnext = None
              for hp in range(2):
                  w = wB0 if hp == 0 else wnext
                  wnext = load_win(1536, 768) if hp == 0 else load_win(2304, 256)
                  QT = AR.alloc(1, NT, BF16)
                  KT = AR.alloc(1, NT, BF16)
                  GS = AR.alloc(1, NT, BF16)
                  KR = AR.alloc(18, 128, BF16)
                  VR = AR.alloc(18, 128, BF16)
                  STB = AR.alloc(36, 64, BF16)
                  ST32 = AR.alloc(2, 64, F32)
                  mB = AR.mark()
                  VfT = AR.alloc(1, NT, BF16)
                  ropet = [AR.alloc(2, 512, F32) for _ in range(1)]
                  ta = [AR.alloc(1, 512, F32) for _ in range(2)]
                  tb = [AR.alloc(1, 512, F32) for _ in range(2)]
                  for bi, (t0, n) in enumerate(BLOCKS):
                      rt = ropet[0]
                      DMA(rt.s(None, 0, n).ap, rope_d[:, :, t0:t0 + n], [], rt.s(None, 0, n).tok)
                      for qi, (dst, sc) in enumerate(((QT, 1.0), (KT, 0.125))):
                          b0 = proj(w, qi * 128, t0, n)
                          b1 = proj(w, (4 + qi) * 128, t0, n)
                          a_, b_ = ta[qi], tb[qi]
                          TT(a_.s(0, 0, n), PB(b0, 0, n), rt.s(0, 0, n), ALU.mult)
                          TT(b_.s(0, 0, n), PB(b1, 0, n), rt.s(1, 0, n), ALU.mult)
                          TT(a_.s(0, 0, n), a_.s(0, 0, n), b_.s(0, 0, n), ALU.add, eng="pool")
                          ACT(dst.s(0, t0, t0 + n), a_.s(0, 0, n), AF.Identity, scale=sc)
                      bv = proj(w, 2 * 128, t0, n)
                      ACT(VfT.s(0, t0, t0 + n), PB(bv, 0, n), AF.Copy)
                      bg = proj(w, 3 * 128, t0, n)
                      ACT(GS.s(0, t0, t0 + n), PB(bg, 0, n), AF.Silu)
                  chk("B1")
                  for src, dst in ((KT, KR), (VfT, VR)):
                      for c0 in range(0, 18, 4):
                          cn = min(4, 18 - c0)
                          b = bank()
                          TRS([(PBh(b, i * 128, i * 128 + 128), src.s(0, (c0 + i) * 128, (c0 + i + 1) * 128)) for i in range(cn)], ident)
                          o = V(dst.ap3[:, c0:c0 + cn, :], dst.s((c0, c0 + cn)).tok)
                          i_ = V(PS16[:, b * 1024:b * 1024 + cn * 128].rearrange("p (a c) -> p a c", a=cn), PBh(b, 0, 1).tok)
                          ACT(o, i_, AF.Copy)
                  chk("B2")
                  AR.release(mB)
                  vdec = [AR.alloc(1, 128, BF16) for _ in range(4)]
                  sm = [AR.alloc(2, 128, BF16) for _ in range(3)]
                  qdec = [AR.alloc(2, 128, BF16) for _ in range(3)]
                  yr = [AR.alloc(1, 512, F32) for _ in range(2)]
                  yb16 = [AR.alloc(1, 512, BF16) for _ in range(2)]
                  dsq = [AR.alloc(1, 512, BF16) for _ in range(2)]
                  rsq = [AR.alloc(1, 512, F32) for _ in range(2)]
                  yo = AR.alloc(1, NT, BF16)
                  qzb = [AR.alloc(1, 128, BF16) for _ in range(3)]
                  for qz_ in qzb:
                      MEMSET(qz_.all(), 0.0, eng="pool")
                  for dr in range(2):
                      order = list(range(18)) if dr == 0 else [1, 0] + list(range(17, 1, -1))
                      st = ST32.s(dr)
                      MEMSET(st, 0.0)
                      for ci, c in enumerate(order):
                          COPY(STB.s(dr * 18 + c), st, eng="pool")
                          if ci == 17:
                              break
                          vd = vdec[(dr * 18 + ci) % 4]
                          vdv = VD.s(0, dr * 4 + hp * 2, dr * 4 + hp * 2 + 2)
                          TT(V(vd.flat.rearrange("p (h e) -> p h e", h=2), vd.all().tok),
                             V(VR.ap3[:, c, :].rearrange("p (h e) -> p h e", h=2), VR.s(c).tok),
                             V(vdv.ap.unsqueeze(2).to_broadcast([128, 2, 64]), vdv.tok), ALU.mult)
                          b = bank()
                          MM(PB(b, 0, 128), [(KR.s(c), vd.all())])
                          for hh in range(2):
                              STT(st.p(hh * 64, hh * 64 + 64), st.p(hh * 64, hh * 64 + 64), LG.s(0, 12 + hp * 2 + dr, 13 + hp * 2 + dr).p(hh * 64, hh * 64 + 64),
                                  PB(b, hh * 64, hh * 64 + 64).p(hh * 64, hh * 64 + 64), ALU.mult, ALU.add)
                  chk("B3")
                  def r1(c):
                      cs, ce = c * 128, (c + 1) * 128
                      qz = qzb[c % 3]
                      COPY(qz.all().p(0, 64), QT.s(0, cs, ce).p(0, 64), eng="pool")
                      qd = qdec[c % 3]
                      for dr in range(2):
                          TT(qd.s(dr), QT.s(0, cs, ce), GQ.s(hp * 2 + dr), ALU.mult, eng="pool")
                      bs = c % 3
                      MMG([(PB(bs, 0, 128), [(KT.s(0, cs, ce), qz.all())]),
                           (PB(bs, 128, 256), [(KT.s(0, cs, ce).p(64, 128), QT.s(0, cs, ce).p(64, 128))])])
                      smt = sm[c % 3]
                      ddv = DD.s((hp * 2, hp * 2 + 2))
                      TT(smt.all(), V(PS32[:, bs * 512:bs * 512 + 256].rearrange("p (h i) -> p h i", h=2), PB(bs, 0, 256).tok), ddv, ALU.mult)

                  def r2(c):
                      blk0 = (c // 4) * 4
                      ci = c - blk0
                      par = (blk0 // 4) % 2
                      yrt = yr[par]
                      smt = sm[c % 3]
                      qd = qdec[c % 3]
                      by = 3 + c % 3
                      groups = []
                      for hh in range(2):
                          ps_, pe_ = hh * 64, hh * 64 + 64
                          terms = [(V(VR.ap3[:, c, ps_:pe_], VR.s(c).tok), smt.s(hh)),
                                   (STB.s(0 * 18 + c).p(ps_, pe_), qd.s(0).p(ps_, pe_)),
                                   (STB.s(1 * 18 + c).p(ps_, pe_), qd.s(1).p(ps_, pe_))]
                          groups.append((PB(by, 0, 128).p(ps_, pe_), terms))
                      MMG(groups)
                      ACT(yrt.s(0, ci * 128, ci * 128 + 128), PB(by, 0, 128), AF.Copy)
                      if c == 17 or ci == 3:
                          cn = ci + 1
                          t0b = blk0 * 128
                          nb_ = cn * 128
                          y16 = yb16[par]
                          dq = dsq[par]
                          ACT(y16.s(0, 0, nb_), yrt.s(0, 0, nb_), AF.Copy)
                          MM(PB(6, 0, nb_), [(bd64, y16.s(0, 0, nb_))])
                          TT(yrt.s(0, 0, nb_), yrt.s(0, 0, nb_), PB(6, 0, nb_), ALU.subtract)
                          ACT(dq.s(0, 0, nb_), yrt.s(0, 0, nb_), AF.Square)
                          MM(PB(7, 0, nb_), [(bd64, dq.s(0, 0, nb_))])
                          rsv = rsq[par].s(0, 0, nb_)
                          ACT(rsv, PB(7, 0, nb_), AF.Sqrt, bias=epsc, scale=1.0)
                          RECIP(rsv, rsv)
                          TT(yrt.s(0, 0, nb_), yrt.s(0, 0, nb_), rsv, ALU.mult)
                          TT(yo.s(0, t0b, t0b + nb_), yrt.s(0, 0, nb_), GS.s(0, t0b, t0b + nb_), ALU.mult, eng="pool")

                  for step in range(19):
                      if step < 18:
                          r1(step)
                      if step >= 1:
                          r2(step - 1)
                  bank_ctr[0] = 0
                  DMA(yd[2 + hp], yo.all().ap, yo.all().tok, ydtok(2 + hp, 0, NT))
                  AR.release(G0)
              wC = wnext

              chk("B")
              wD0 = load_win(2560, 384)
              PCS = AR.alloc(18, 512, BF16)
              ys_t = [AR.alloc(2, 256, BF16) for _ in range(2)]
              ysi = [0]

              def yc_store(bj_, c0):
                  yt = ys_t[ysi[0] % 2]
                  ysi[0] += 1
                  for j in range(2):
                      if j:
                          ACT(yt.s(j), PB(bj_[j], 0, 256), AF.Copy)
                      else:
                          COPY(yt.s(j), PB(bj_[j], 0, 256))
                      DMA(yd[4 + j][:, c0:c0 + 256], yt.s(j).ap, yt.s(j).tok, ydtok(4 + j, c0, c0 + 256))
              mC = AR.mark()
              PT = [AR.alloc(1, NT, BF16) for _ in range(2)]
              for j in range(2):
                  for (t0, n) in BLOCKS:
                      b = proj(wC, j * 128, t0, n)
                      ACT(PT[j].s(0, t0, t0 + n), PB(b, 0, n), AF.Copy)
              for tc in range(18):
                  b = bank()
                  MMG([(PB(b, j * 256, j * 256 + 256), [(PT[j].s(0, tc * 128, tc * 128 + 128), cs2)]) for j in range(2)])
                  if tc % 2:
                      ACT(PCS.s(tc), PB(b, 0, 512), AF.Copy)
                  else:
                      COPY(PCS.s(tc), PB(b, 0, 512))
              AR.release(mC)
              dbuf = [AR.alloc(16, 256, BF16) for _ in range(3)]
              bj = [bank(), bank()]
              groups = []
              for j in range(2):
                  terms = []
                  for ncx in range(2):
                      terms.append((PCS.s(ncx, j * 256, j * 256 + 128), CB2.s(0, ncx * 256, ncx * 256 + 256)))
                      terms.append((PCS.s(ncx, j * 256 + 128, j * 256 + 256), CB2.s(0, 512 + ncx * 256, 512 + ncx * 256 + 256)))
                  groups.append((PB(bj[j], 0, 256), terms))
              MMG(groups)
              yc_store(bj, 0)
              di = 0
              for nb8 in range(8):
                  dc_ = dbuf[di % 3]
                  di += 1
                  ds_ = dbuf[di % 3]
                  di += 1
                  DMA(dc_.all().ap, dftc_d[nb8], [], dc_.all().tok)
                  DMA(ds_.all().ap, dfts_d[nb8], [], ds_.all().tok)
                  bj = [bank(), bank()]
                  MMG([(PB(bj[j], 0, 256), [(PCS.s(2 + ncx, j * 256, j * 256 + 128), dc_.s(ncx)) for ncx in range(16)], True, False) for j in range(2)])
                  MMG([(PB(bj[j], 0, 256), [(PCS.s(2 + ncx, j * 256 + 128, j * 256 + 256), ds_.s(ncx)) for ncx in range(16)], False, True) for j in range(2)])
                  yc_store(bj, 256 + nb8 * 256)
              AR.release(G0)

              chk("C")
              for hp in range(2):
                  w = wD0 if hp == 0 else wnext
                  if hp == 0:
                      wnext = load_win(2944, 384)
                  QT = AR.alloc(1, NT, BF16)
                  KT = AR.alloc(1, NT, BF16)
                  VfT = AR.alloc(1, NT, BF16)
                  VR = AR.alloc(18, 128, BF16)
                  BM = AR.alloc(10, 640, BF16)
                  EX = [AR.alloc(1, 896, BF16) for _ in range(3)]
                  PTs = [AR.alloc(1, 896, BF16) for _ in range(3)]
                  osb = [AR.alloc(1, 128, BF16) for _ in range(3)]
                  st_ = [AR.alloc(1, 8, F32) for _ in range(3)]
                  yo = AR.alloc(1, NT, BF16)
                  qzb = [AR.alloc(1, 128, BF16) for _ in range(3)]
                  for qz_ in qzb:
                      MEMSET(qz_.all(), 0.0, eng="pool")
                  for var in range(5):
                      v = BM.s((var * 2, var * 2 + 2))
                      DMA(v.ap, natb_d[l, var, hp * 2:hp * 2 + 2].rearrange("h p k -> p h k"), [], v.tok, eng="pool")
                  for (t0, n) in BLOCKS:
                      b0 = proj(w, 0, t0, n)
                      ACT(QT.s(0, t0, t0 + n), PB(b0, 0, n), AF.Identity, scale=0.125)
                      b1 = proj(w, 128, t0, n)
                      ACT(KT.s(0, t0, t0 + n), PB(b1, 0, n), AF.Copy)
                      b2 = proj(w, 256, t0, n)
                      ACT(VfT.s(0, t0, t0 + n), PB(b2, 0, n), AF.Copy)
                  for c0 in range(0, 18, 4):
                      cn = min(4, 18 - c0)
                      b = bank()
                      TRS([(PBh(b, i * 128, i * 128 + 128), VfT.s(0, (c0 + i) * 128, (c0 + i + 1) * 128)) for i in range(cn)], ident)
                      o = V(VR.ap3[:, c0:c0 + cn, :], VR.s((c0, c0 + cn)).tok)
                      i_ = V(PS16[:, b * 1024:b * 1024 + cn * 128].rearrange("p (a c) -> p a c", a=cn), PBh(b, 0, 1).tok)
                      ACT(o, i_, AF.Copy)
                  iters = [(qt, hh) for qt in range(-2 if l < DEPTH - 1 else 0, 16) for hh in range(2)]

                  def ngeo(qt):
                      if qt < 0:
                          return (qt + 2) * 128, 256, [0, 1], 0, 0
                      kr0e, var = nat_geom(qt)
                      kc0 = 2 + kr0e // 2
                      return NCTX + qt * 128, 896, [kc0 + i for i in range(5)] + [0, 1], kc0, var

                  def s1(i):
                      qt, hh = iters[i]
                      qs_, nk, kchunks, kc0, var = ngeo(qt)
                      qz = qzb[(qt + 2) % 3]
                      if hh == 0:
                          COPY(qz.all().p(0, 64), QT.s(0, qs_, qs_ + 128).p(0, 64), eng="pool")
                          qv = qz.all()
                          kv_ = lambda a_, b_: KT.s(0, a_, b_)
                      else:
                          qv = QT.s(0, qs_, qs_ + 128).p(64, 128)
                          kv_ = lambda a_, b_: KT.s(0, a_, b_).p(64, 128)
                      bsc = 0 if i % 2 == 0 else 2
                      if qt < 0:
                          MM(PB(bsc, 0, 256), [(qv, kv_(0, 256))])
                      else:
                          ks_ = kc0 * 128
                          MMG([
                              (PB(bsc, 0, 512), [(qv, kv_(ks_, ks_ + 512)), (ident, BM.s(var * 2 + hh, 0, 512))]),
                              (PB(bsc + 1, 0, 128), [(qv, kv_(ks_ + 512, ks_ + 640)), (ident, BM.s(var * 2 + hh, 512, 640))]),
                              (PB(bsc + 1, 128, 384), [(qv, kv_(0, 256))]),
                          ])
                      sv = st_[i % 3]
                      ex = EX[i % 3]
                      scv = PB(bsc, 0, nk, nb=2)
                      REDMAXNEG(sv.s(0, 0, 1), scv)
                      ACT(ex.s(0, 0, nk), scv, AF.Exp, bias=sv.s(0, 0, 1), scale=1.0, accum=sv.s(0, 1, 2))
                      RECIP(sv.s(0, 2, 3), sv.s(0, 1, 2))

                  def s2(i):
                      qt, hh = iters[i]
                      qs_, nk, kchunks, kc0, var = ngeo(qt)
                      ex = EX[i % 3]
                      pt = PTs[i % 3]
                      bt = 4 + i % 2
                      nch = nk // 128
                      TRS([(PBh(bt, k * 128, k * 128 + 128), ex.s(0, k * 128, k * 128 + 128)) for k in range(nch)], ident)
                      if i % 3 == 0:
                          ACT(pt.s(0, 0, nk), PBh(bt, 0, nk), AF.Copy)
                      else:
                          COPY(pt.s(0, 0, nk), PBh(bt, 0, nk))

                  def s3(i):
                      qt, hh = iters[i]
                      qs_, nk, kchunks, kc0, var = ngeo(qt)
                      pt = PTs[i % 3]
                      sv = st_[i % 3]
                      ps_, pe_ = hh * 64, hh * 64 + 64
                      nch = nk // 128
                      oc = ((qt + 2) % 2) * 128 + hh * 64
                      MM(PB(6, oc, oc + 64), [(pt.s(0, k * 128, k * 128 + 128), V(VR.ap3[:, kchunks[k], ps_:pe_], VR.s(kchunks[k]).tok)) for k in range(nch)])
                      ot = osb[(qt + 2) % 3]
                      ACT(ot.s(0, hh * 64, hh * 64 + 64), PB(6, oc, oc + 64), AF.Identity, scale=sv.s(0, 2, 3))
                      if hh == 1:
                          TRS([(PBh(7, ((qt + 2) % 2) * 128, ((qt + 2) % 2) * 128 + 128), ot.all())], ident)
                          COPY(yo.s(0, qs_, qs_ + 128), PBh(7, ((qt + 2) % 2) * 128, ((qt + 2) % 2) * 128 + 128))

                  NI = len(iters)
                  for step in range(NI + 2):
                      if step < NI:
                          s1(step)
                      if 0 <= step - 1 < NI:
                          s2(step - 1)
                      if 0 <= step - 2 < NI:
                          s3(step - 2)
                  bank_ctr[0] = 0
                  DMA(yd[6 + hp], yo.all().ap, yo.all().tok, ydtok(6 + hp, 0, NT))
                  AR.release(G0)

              chk("D")
              wo = WB[wb_ctr[0] % 2]
              wb_ctr[0] += 1
              DMA(wo.all().ap, wout_d[l], [], wo.all().tok, eng="pool")
              yb = [AR.alloc(8, 512, BF16) for _ in range(2)]
              ob = AR.alloc(8, 512, F32)
              sq = AR.alloc(8, 512, BF16)
              rs = AR.alloc(1, 512, F32)
              last_layer = (l == DEPTH - 1)
              for bi, (t0, n) in enumerate(BLOCKS):
                  if last_layer and bi == 0:
                      continue
                  which = 1 if bi == 0 else 0
                  ybt = yb[bi % 2]
                  v = ybt.s(None, 0, n)
                  rd = []
                  for c in range(8):
                      rd += ydtok(c, t0, t0 + n)
                  DMA(v.ap, yd[:, :, t0:t0 + n].rearrange("c p t -> p c t"), rd, v.tok)
                  for dc in range(8):
                      b = bank()
                      MM(PB(b, 0, n), [(wo.s(m, dc * 128, dc * 128 + 128), ybt.s(m, 0, n)) for m in range(8)])
                      if dc % 2:
                          ACT(ob.s(dc, 0, n), PB(b, 0, n), AF.Copy)
                      else:
                          COPY(ob.s(dc, 0, n), PB(b, 0, n))
                  resid_update(ob, t0, n, which, 2, sq, rs)
              AR.release(M0)
              if dbg == ("mix", l):
                  break

              F0 = AR.mark()
              HX2 = AR.alloc(8, 514, BF16)
              G = AR.alloc(22, 512, BF16)
              WUP = [AR.alloc(8, 256, BF16) for _ in range(4)]
              WDN = AR.alloc(22, 1024, BF16)
              DMA(WDN.all().ap, wdn_d[l], [], WDN.all().tok, eng="pool")
              WMB = [AR.alloc(8, 256, BF16) for _ in range(2)]
              do_mod = (l + 1 < nlayers)

              def mod_dma(p):
                  wm = WMB[p % 2]
                  DMA(wm.all().ap, wmod_d[l + 1, :, :, p * 256:(p + 1) * 256], [], wm.all().tok, eng="pool")

              def mod_compute(p):
                  wm = WMB[p % 2]
                  b = bank()
                  MMG([(PB(b, ch * 2, ch * 2 + 2), [(wm.s(kc, ch * 128, ch * 128 + 128), SCB.s(kc)) for kc in range(8)]) for ch in range(2)])
                  o = V(MOD.ap3[:, l + 1, p * 4:p * 4 + 4].rearrange("p (c w) -> p c w", w=2), MOD.s(l + 1, p * 4, p * 4 + 4).tok)
                  i0 = V(PS32[:, b * 512:b * 512 + 4].rearrange("p (c w) -> p c w", w=2), PB(b, 0, 4).tok)
                  bm = SPAR.s(l + 1, SP_BMOD + p * 2, SP_BMOD + p * 2 + 2)
                  i1 = V(bm.ap.unsqueeze(2).to_broadcast([128, 2, 2]), bm.tok)
                  TT(o, i0, i1, ALU.add)
              if do_mod:
                  mod_dma(0)
                  mod_dma(1)
              FS = AR.mark()
              ffn_blocks = [bi_ for bi_ in range(len(BLOCKS)) if not (last_layer and bi_ == 0)]
              wu_seq = [(bi_, j_) for bi_ in ffn_blocks for j_ in range(22)]
              wu_issued = [0]

              def wu_prefetch(upto):
                  while wu_issued[0] < min(upto, len(wu_seq)):
                      k_ = wu_issued[0]
                      wt_ = WUP[k_ % 4]
                      DMA(wt_.all().ap, wup_d[l, wu_seq[k_][1]], [], wt_.all().tok, eng="pool")
                      wu_issued[0] += 1
              wu_prefetch(3)
              for fi, bi in enumerate(ffn_blocks):
                  t0, n = BLOCKS[bi]
                  which = 1 if bi == 0 else 0
                  seq_start = t0 in (0, NCTX)
                  seq_end = (t0 + n) in (NCTX, NT)
                  sq = AR.alloc(8, 514, BF16)
                  rs = AR.alloc(1, 514, F32)
                  t1s = [AR.alloc(1, 514, F32) for _ in range(2)]
                  if seq_start:
                      MEMSET(HX2.s(None, 0, 1), 0.0)
                  else:
                      COPY(HX2.s(None, 0, 1), HALO.s(None, 0, 1))
                  rmsnorm_to(lambda kc: HX2.s(kc, 1, 1 + n), t0, n, which, 3, 4, sq, rs, t1s)
                  if seq_end:
                      MEMSET(HX2.s(None, n + 1, n + 2), 0.0)
                  else:
                      rmsnorm_to(lambda kc: HX2.s(kc, n + 1, n + 2), t0 + n, 1, which, 3, 4, sq, rs, t1s)
                  COPY(HALO.s(None, 0, 1), HX2.s(None, n, n + 1))
                  AR.release(FS)
                  accA = [AR.alloc(2, 256, F32) for _ in range(2)]
                  accB = [AR.alloc(2, 256, F32) for _ in range(2)]
                  sa = [AR.alloc(2, 256, F32) for _ in range(2)]
                  ncb = 2 if n == 512 else 1
                  cw_ = n // ncb
                  for j in range(22):
                      wu = WUP[(fi * 22 + j) % 4]
                      wu_prefetch(fi * 22 + j + 4)
                      g_ = fi * 22 + j
                      if do_mod and g_ % 4 == 0 and g_ // 4 < 24:
                          mod_compute(g_ // 4)
                          if g_ // 4 + 2 < 24:
                              mod_dma(g_ // 4 + 2)
                      ba = bank2()
                      bb = bank2()
                      groups = []
                      for (bk, wc0) in ((ba, 0), (bb, 128)):
                          for cbk in range(ncb):
                              c0 = cbk * cw_
                              groups.append((PB(bk + cbk, 0, cw_ + 2), [(wu.s(kc, wc0, wc0 + 128), HX2.s(kc, c0, c0 + cw_ + 2)) for kc in range(8)]))
                      MMG(groups)
                      for (bk, accs, ch) in ((ba, accA, j), (bb, accB, 22 + j)):
                          at = accs[j % 2]
                          av = V(at.ap3[:, 0:ncb, 0:cw_], at.all().tok)
                          fw = lambda k: SPAR.s(l, SP_FCW + ch * 3 + k, SP_FCW + ch * 3 + k + 1)
                          ACT(av, PB3(bk, ncb, 1, cw_ + 1), AF.Identity, scale=fw(1))
                          STT(av, PB3(bk, ncb, 0, cw_), fw(0), av, ALU.mult, ALU.add)
                          STT(av, PB3(bk, ncb, 2, cw_ + 2), fw(2), av, ALU.mult, ALU.add)
                      sat = sa[j % 2]
                      sav = V(sat.ap3[:, 0:ncb, 0:cw_], sat.all().tok)
                      aAv = V(accA[j % 2].ap3[:, 0:ncb, 0:cw_], accA[j % 2].all().tok)
                      aBv = V(accB[j % 2].ap3[:, 0:ncb, 0:cw_], accB[j % 2].all().tok)
                      ACT(sav, aAv, AF.Silu)
                      gv = V(G.ap3[:, j, 0:n].rearrange("p (a c) -> p a c", a=ncb), G.s(j, 0, n).tok)
                      TT(gv, sav, aBv, ALU.mult, eng="pool")
                  AR.release(FS)
                  ob = AR.alloc(8, 512, F32)
                  sq = HX2
                  rs = AR.alloc(1, 512, F32)
                  for dc in range(8):
                      b = bank()
                      MM(PB(b, 0, n), [(WDN.s(j, dc * 128, dc * 128 + 128), G.s(j, 0, n)) for j in range(22)])
                      if dc % 2:
                          ACT(ob.s(dc, 0, n), PB(b, 0, n), AF.Copy)
                      else:
                          COPY(ob.s(dc, 0, n), PB(b, 0, n))
                  resid_update(ob, t0, n, which, 5, sq, rs)
                  AR.release(FS)
              AR.release(F0)

        except _Stop:
            pass
        finals = []
        for kc in range(8):
            v = XT.s(kc, NCTX, NT)
            finals.append(DMA(yout[:, kc, :], v.ap, v.tok, []))
        if dbg:
            for kc in range(8):
                v = XT.s(kc)
                finals.append(DMA(xdbg[:, kc, :], v.ap, v.tok, []))
            for rec in S.recs.get("yd", []):
                if rec[2] is not None:
                    finals.append(rec[2])
        S.emit(nc, finals)
    build_program.last_stats = (S.nops, AR.peak)
    return nc


_CONST_CACHE = {}


def _consts():
    if _CONST_CACHE:
        return _CONST_CACHE
    bf = ml_dtypes.bfloat16
    cb = np.zeros((128, CB_N), np.float32)
    cb[:, CB_ID:CB_ID + 128] = np.eye(128)
    cb[:, CB_ONE:CB_ONE + 128] = 1.0
    bd = np.zeros((128, 128), np.float32)
    bd[:64, :64] = 1.0 / 64
    bd[64:, 64:] = 1.0 / 64
    cb[:, CB_BD:CB_BD + 128] = bd
    cc = np.arange(64)
    ang = 2 * np.pi * ((cc[:, None] * cc[None, :]) % 64) / 64.0
    Cc, Sc = np.cos(ang) / 8.0, np.sin(ang) / 8.0
    z = np.zeros((64, 64))
    cb[:, CB_CS2:CB_CS2 + 128] = np.block([[Cc, z], [z, Cc]])
    cb[:, CB_CS2 + 128:CB_CS2 + 256] = -np.block([[Sc, z], [z, Sc]])
    _CONST_CACHE["cb"] = cb.astype(bf)

    def dft(n):
        k = np.arange(n)
        a = 2 * np.pi * ((k[:, None] * k[None, :]) % n) / float(n)
        return np.cos(a) / np.sqrt(n), np.sin(a) / np.sqrt(n)
    c256, s256 = dft(256)
    cb2 = np.zeros((128, 1024), np.float32)
    for ncx in range(2):
        cb2[:, ncx * 256:(ncx + 1) * 256] = c256[ncx * 128:(ncx + 1) * 128, :]
        cb2[:, 512 + ncx * 256:512 + (ncx + 1) * 256] = s256[ncx * 128:(ncx + 1) * 128, :]
    _CONST_CACHE["cb2"] = cb2.astype(bf)
    c2k, s2k = dft(2048)
    _CONST_CACHE["dftc"] = np.ascontiguousarray(c2k.reshape(16, 128, 8, 256).transpose(2, 1, 0, 3)).astype(bf)
    _CONST_CACHE["dfts"] = np.ascontiguousarray(s2k.reshape(16, 128, 8, 256).transpose(2, 1, 0, 3)).astype(bf)
    cf = np.zeros((128, CF_N), np.float32)
    jj = np.arange(128)[:, None]
    ii = np.arange(128)[None, :]
    BIG = 1.0e5
    cf[:, CF_DF:CF_DF + 128] = np.where(ii >= jj, ii - jj, BIG)
    cf[:, CF_DB:CF_DB + 128] = np.where(jj >= ii, jj - ii, BIG)
    cf[:, CF_IL1:CF_IL1 + 128] = ii + 1.0
    cf[:, CF_ILB:CF_ILB + 128] = 128.0 - ii
    cf[:, CF_JL] = 127.0 - np.arange(128)
    cf[:, CF_JL + 1] = np.arange(128)
    cf[:, CF_EPS] = EPS
    _CONST_CACHE["cf"] = cf
    t = np.arange(NX)
    row = (t // 64).astype(np.float32)
    col = (t % 64).astype(np.float32)
    inv = (np.float32(10000.0) ** (-np.arange(16, dtype=np.float32) / np.float32(16))).astype(np.float32)
    angx = np.concatenate([row[:, None] * inv, col[:, None] * inv], -1).astype(np.float32)
    cosx, sinx = np.cos(angx).astype(np.float32), np.sin(angx).astype(np.float32)
    rope = np.zeros((128, 2, NT), np.float32)
    rope[:, 0, :NCTX] = 1.0
    for p in range(128):
        d = p % 64
        f = d % 32
        rope[p, 0, NCTX:] = cosx[:, f]
        rope[p, 1, NCTX:] = -sinx[:, f] if d < 32 else sinx[:, f]
    _CONST_CACHE["rope"] = rope
    return _CONST_CACHE


def _win_cols():
    cols = []
    for j in range(2):
        for part in range(3):
            cols.append(np.arange(128) + part * 256 + j * 128)
    for hp in range(2):
        base = 768
        for part in range(4):
            cols.append(base + part * 256 + hp * 128 + np.arange(128))
        for part in range(2):
            p = np.arange(128)
            cols.append(base + part * 256 + hp * 128 + (p // 64) * 64 + ((p % 64) + 32) % 64)
    for j in range(2):
        cols.append(1792 + j * 128 + np.arange(128))
    for hp in range(2):
        for part in range(3):
            cols.append(2048 + part * 256 + hp * 128 + np.arange(128))
    return np.concatenate(cols)


def _nat_tables(rpb_l):
    out = np.full((5, 4, 128, 640), -1.0e30, np.float32)
    for vi, qt in enumerate([2, 0, 1, 14, 15]):
        s0 = min(max(2 * qt - 4, 0), 24)
        kr0e = min(s0 - s0 % 2, 22)
        for rr in range(2):
            r = 2 * qt + rr
            start = min(max(r - 4, 0), 24)
            for cq in range(64):
                q = rr * 64 + cq
                qcs = min(max(cq - 8, 0), 48)
                kc = np.arange(qcs, qcs + 16)
                dcol = np.clip(kc - cq + 15, 0, 30)
                for kr in range(10):
                    keyrow = kr0e + kr
                    if not (start <= keyrow < start + 8):
                        continue
                    drow = keyrow - r + 7
                    out[vi, :, q, kr * 64 + qcs:kr * 64 + qcs + 16] = rpb_l[:, drow, dcol]
    return out


def _fm(v):
    return np.ascontiguousarray(v.reshape(-1, 128).T)


def host_prep(inp):
    f32 = np.float32
    C = _consts()
    shared = {k: C[k] for k in ("cb", "cb2", "cf", "rope", "dftc", "dfts")}
    spar = np.zeros((128, DEPTH, SP_N), f32)
    for l in range(DEPTH):
        spar[:, l, SP_GPM:SP_GPM + 8] = _fm(inp["g_pre_mix"][l])
        spar[:, l, SP_GPO:SP_GPO + 8] = _fm(inp["g_post_mix"][l])
        spar[:, l, SP_GPF:SP_GPF + 8] = _fm(inp["g_pre_ffn"][l])
        spar[:, l, SP_GPOF:SP_GPOF + 8] = _fm(inp["g_post_ffn"][l])
        spar[:, l, SP_BMOD:SP_BMOD + 48] = _fm(inp["b_mod"][l])
        cw = inp["conv_w"][l]
        for j in range(2):
            for k in range(3):
                spar[:, l, SP_CW + j * 3 + k] = cw[k, j * 128:(j + 1) * 128]
        fw = inp["ffn_conv_w"][l]
        for ch in range(44):
            for k in range(3):
                spar[:, l, SP_FCW + ch * 3 + k] = fw[k, ch * 128:(ch + 1) * 128]
        rd = inp["ret_decay"][l]
        for dr in range(2):
            for h in range(4):
                spar[:, l, SP_RDB + dr * 4 + h] = rd[dr, h]
        for hp in range(2):
            for dr in range(2):
                spar[:64, l, SP_RDP + hp * 2 + dr] = rd[dr, 2 * hp]
                spar[64:, l, SP_RDP + hp * 2 + dr] = rd[dr, 2 * hp + 1]
    shared["spar"] = spar
    kmaj = lambda w: np.ascontiguousarray(w.reshape(8, 128, -1).transpose(1, 0, 2))
    cols = _win_cols()
    shared["wmod"] = np.stack([kmaj(np.asarray(inp["w_mod"][l], f32)) for l in range(DEPTH)])
    shared["win"] = np.stack([kmaj(np.asarray(inp["w_in"][l], f32)[:, cols]) for l in range(DEPTH)])
    shared["wout"] = np.stack([kmaj(np.asarray(inp["w_out"][l], f32)) for l in range(DEPTH)])
    wup = np.zeros((DEPTH, 22, 128, 8, 256), f32)
    for l in range(DEPTH):
        w = kmaj(np.asarray(inp["w_up"][l], f32))
        for j in range(22):
            wup[l, j, :, :, 0:128] = w[:, :, j * 128:(j + 1) * 128]
            wup[l, j, :, :, 128:256] = w[:, :, DFF + j * 128:DFF + (j + 1) * 128]
    shared["wup"] = wup
    shared["wdn"] = np.stack([np.ascontiguousarray(np.asarray(inp["w_down"][l], f32).reshape(22, 128, 1024).transpose(1, 0, 2)) for l in range(DEPTH)])
    shared["natb"] = np.stack([_nat_tables(np.asarray(inp["nat_rpb"][l], f32)) for l in range(DEPTH)])
    per_core = []
    x = np.asarray(inp["x"], f32)
    ctx = np.asarray(inp["ctx"], f32)
    c = np.asarray(inp["c"], f32)
    cc = _fm(np.asarray(inp["c_ctx"], f32))
    for b in range(x.shape[0]):
        cat = np.concatenate([ctx[b], x[b]], 0)
        xin = np.ascontiguousarray(cat.T.reshape(8, 128, NT).transpose(1, 0, 2))
        cvec = np.concatenate([_fm(c[b]), cc], 1).astype(f32)
        per_core.append({"xin": xin, "cvec": cvec})
    return shared, per_core


_PROG = {}


def kernel(**inputs):
    shared, per_core = host_prep(inputs)
    if "nc" not in _PROG:
        _PROG["nc"] = build_program()
    nc = _PROG["nc"]
    n = len(per_core)
    in_maps = [dict(shared, **pc) for pc in per_core]
    res = run_bass_kernel_spmd(nc, in_maps, core_ids=list(range(n)))
    out = np.zeros((n, NX, D), np.float32)
    for b in range(n):
        y = res.results[b]["yout"]
        out[b] = y.transpose(2, 1, 0).reshape(NX, D)
    return out
```

```python
import contextlib
import numpy as np
import ml_dtypes
import concourse.bass as bass
import concourse.mybir as mybir
from concourse.bass_utils import run_bass_kernel_spmd

F32 = mybir.dt.float32
BF16 = mybir.dt.bfloat16
AF = mybir.ActivationFunctionType
ALU = mybir.AluOpType
AX = mybir.AxisListType

D = 1024
NT = 2304
NCTX = 256
NX = 2048
DEPTH = 4
DFF = 2816
EPS = 1e-6
NWIN = 26 * 128
BLOCKS = [(0, 256), (256, 512), (768, 512), (1280, 512), (1792, 512)]

SP_GPM, SP_GPO, SP_GPF, SP_GPOF, SP_BMOD, SP_CW, SP_FCW, SP_RDB, SP_RDP, SP_N = 0, 8, 16, 24, 32, 80, 86, 218, 226, 230
CF_DF, CF_DB, CF_IL1, CF_ILB, CF_JL, CF_EPS, CF_N = 0, 128, 256, 384, 512, 514, 516
CB_ID, CB_ONE, CB_BD, CB_CS2, CB_N = 0, 128, 256, 384, 640


class Sched:
    ENGS = ("pe", "act", "dve", "pool", "sp")

    def __init__(self, nslots_hw=32, nslots_sw=24):
        self.q = {e: [] for e in self.ENGS}
        self.cnt = {e: 0 for e in self.ENGS}
        self.seen = {e: {} for e in self.ENGS}
        self.recs = {}
        self.nslots_hw, self.nslots_sw = nslots_hw, nslots_sw
        self.nslots = nslots_hw + nslots_sw
        self.slot_cnt = [0] * self.nslots
        self.next_hw = 0
        self.next_sw = 0
        self.nops = 0

    def add(self, eng, fn, reads=(), writes=(), dma=False):
        writes = list(writes) + [t for t in reads if t[0] == "ps"]
        deps = {}
        for (space, lo, hi) in reads:
            for rec in self.recs.setdefault(space, []):
                if rec[0] < hi and lo < rec[1] and rec[2] is not None:
                    s, v = rec[2]
                    if deps.get(s, 0) < v:
                        deps[s] = v
        for (space, lo, hi) in writes:
            for rec in self.recs.setdefault(space, []):
                if rec[0] < hi and lo < rec[1]:
                    if rec[2] is not None:
                        s, v = rec[2]
                        if deps.get(s, 0) < v:
                            deps[s] = v
                    for s, v in rec[3].items():
                        if deps.get(s, 0) < v:
                            deps[s] = v
        if dma:
            if eng == "pool":
                sl = self.nslots_hw + self.next_sw
                self.next_sw = (self.next_sw + 1) % self.nslots_sw
            else:
                sl = self.next_hw
                self.next_hw = (self.next_hw + 1) % self.nslots_hw
            sem = ("dma", sl)
            if self.slot_cnt[sl] > 0:
                v = 16 * self.slot_cnt[sl]
                if deps.get(sem, 0) < v:
                    deps[sem] = v
            self.slot_cnt[sl] += 1
            handle = (sem, 16 * self.slot_cnt[sl])
        else:
            self.cnt[eng] += 1
            handle = (("eng", eng), self.cnt[eng])
        waits = []
        seen = self.seen[eng]
        for sem, val in deps.items():
            if sem == ("eng", "pe") and eng == "pe":
                continue
            if seen.get(sem, 0) >= val:
                continue
            seen[sem] = val
            waits.append((sem, val))
        self.q[eng].append((waits, fn, handle))
        self.nops += 1
        for (space, lo, hi) in reads:
            recs = self.recs[space]
            for rec in recs:
                if rec[0] == lo and rec[1] == hi:
                    if rec[3].get(handle[0], 0) < handle[1]:
                        rec[3][handle[0]] = handle[1]
                    break
            else:
                recs.append([lo, hi, None, {handle[0]: handle[1]}])
        for (space, lo, hi) in writes:
            recs = self.recs[space]
            recs[:] = [r for r in recs if not (lo <= r[0] and r[1] <= hi)]
            recs.append([lo, hi, handle, {}])
        return handle

    def emit(self, nc, final_waits):
        with contextlib.ExitStack() as es:
            sems = {}
            for e in self.ENGS:
                sems[("eng", e)] = es.enter_context(nc.semaphore("s_" + e))
            for s in range(self.nslots):
                sems[("dma", s)] = es.enter_context(nc.semaphore("d%d" % s))
            block = es.enter_context(nc.Block())

            def run(engname, engobj):
                for (waits, fn, handle) in self.q[engname]:
                    for (sem, val) in waits:
                        engobj.wait_ge(sems[sem], val)
                    inst = fn(engobj)
                    inst.then_inc(sems[handle[0]], 16 if handle[0][0] == "dma" else 1)
                if engname == "sp":
                    for (sem, val) in final_waits:
                        engobj.wait_ge(sems[sem], val)

            @block.tensor
            def _(e):
                run("pe", e)

            @block.scalar
            def _(e):
                run("act", e)

            @block.vector
            def _(e):
                run("dve", e)

            @block.gpsimd
            def _(e):
                run("pool", e)

            @block.sync
            def _(e):
                run("sp", e)


class V:
    __slots__ = ("ap", "tok")

    def __init__(self, ap, tok):
        self.ap = ap
        self.tok = tok

    def p(self, lo, hi):
        return V(self.ap[lo:hi], self.tok)


class Tile:
    def __init__(self, arena_ap, off, A, B, dt, space="sb"):
        self.es = 4 if dt == F32 else 2
        self.A, self.B, self.dt, self.off, self.space = A, B, dt, off, space
        nbytes = A * B * self.es
        assert off % 4 == 0
        ap = arena_ap[:, off // 4:(off + nbytes + 3) // 4]
        if dt != F32:
            ap = ap.bitcast(dt)[:, 0:A * B]
        self.flat = ap
        self.ap3 = ap.rearrange("p (a b) -> p a b", a=A) if A > 1 else None
        self.nbytes = nbytes

    def _tok(self, a, lo, hi):
        o = self.off + (a * self.B + lo) * self.es
        return (self.space, o, self.off + (a * self.B + hi) * self.es)

    def s(self, a, lo=0, hi=None):
        if hi is None:
            hi = self.B
        if self.A == 1:
            return V(self.flat[:, lo:hi], [self._tok(0, lo, hi)])
        if a is None:
            return V(self.ap3[:, :, lo:hi], [self._tok(i, lo, hi) for i in range(self.A)])
        if isinstance(a, tuple):
            return V(self.ap3[:, a[0]:a[1], lo:hi], [self._tok(i, lo, hi) for i in range(a[0], a[1])])
        return V(self.ap3[:, a, lo:hi], [self._tok(a, lo, hi)])

    def all(self):
        if self.A == 1:
            return V(self.flat, [(self.space, self.off, self.off + self.nbytes)])
        return V(self.ap3, [(self.space, self.off, self.off + self.nbytes)])


class Arena:
    def __init__(self, ap, nbytes):
        self.ap, self.nbytes, self.top = ap, nbytes, 0
        self.peak = 0

    def alloc(self, A, B, dt):
        es = 4 if dt == F32 else 2
        nb = (A * B * es + 63) // 64 * 64
        off = self.top
        self.top += nb
        self.peak = max(self.peak, self.top)
        assert self.top <= self.nbytes, "SBUF arena overflow: %d > %d" % (self.top, self.nbytes)
        return Tile(self.ap, off, A, B, dt)

    def mark(self):
        return self.top

    def release(self, m):
        self.top = m


def nat_geom(qt):
    s0 = min(max(2 * qt - 4, 0), 24)
    kr0e = min(s0 - s0 % 2, 22)
    var = {0: 1, 1: 2, 14: 3, 15: 4}.get(qt, 0)
    return kr0e, var


class _Stop(Exception):
    pass


def build_program(nlayers=DEPTH, dbg=None, stop=None):
    nc = bass.Bass("TRN2", target_bir_lowering=False)

    def chk(name):
        if stop == name:
            raise _Stop()
    S = Sched()
    dram = {}

    def din(name, shape, dt):
        dram[name] = nc.dram_tensor(name, shape, dt, kind="ExternalInput").ap()
        return dram[name]

    xin = din("xin", [128, 8, NT], F32)
    cvec = din("cvec", [128, 16], F32)
    cb_d = din("cb", [128, CB_N], BF16)
    cb2_d = din("cb2", [128, 1024], BF16)
    cf_d = din("cf", [128, CF_N], F32)
    rope_d = din("rope", [128, 2, NT], F32)
    spar_d = din("spar", [128, DEPTH, SP_N], F32)
    wmod_d = din("wmod", [DEPTH, 128, 8, 6144], F32)
    win_d = din("win", [DEPTH, 128, 8, NWIN], F32)
    wout_d = din("wout", [DEPTH, 128, 8, 1024], F32)
    wup_d = din("wup", [DEPTH, 22, 128, 8, 256], F32)
    wdn_d = din("wdn", [DEPTH, 128, 22, 1024], F32)
    natb_d = din("natb", [DEPTH, 5, 4, 128, 640], F32)
    dftc_d = din("dftc", [8, 128, 16, 256], BF16)
    dfts_d = din("dfts", [8, 128, 16, 256], BF16)
    yout = nc.dram_tensor("yout", [128, 8, NX], F32, kind="ExternalOutput").ap()
    ykind = "ExternalOutput" if dbg else "Internal"
    yd = nc.dram_tensor("yd", [8, 128, NT], BF16, kind=ykind).ap()
    if dbg:
        xdbg = nc.dram_tensor("xdbg", [128, 8, NT], F32, kind="ExternalOutput").ap()

    def ydtok(c, lo, hi):
        return [("yd", (c * NT + lo) * 2, (c * NT + hi) * 2)]

    es = contextlib.ExitStack()
    with es:
        NB = 212000
        arena_t = es.enter_context(nc.sbuf_tensor("arena", [128, NB // 4], F32))
        AR = Arena(arena_t[:], NB)
        ps_t = es.enter_context(nc.psum_tensor("psum", [128, 4096], F32))
        PS32 = ps_t[:]
        PS16 = ps_t[:].bitcast(BF16)
        bank_ctr = [0]

        def bank():
            b = bank_ctr[0]
            bank_ctr[0] = (b + 1) % 8
            return b

        def bank2():
            if bank_ctr[0] % 2:
                bank_ctr[0] = (bank_ctr[0] + 1) % 8
            b = bank_ctr[0]
            bank_ctr[0] = (b + 2) % 8
            return b

        def PB(b, lo, hi, nb=1):
            return V(PS32[:, b * 512 + lo:b * 512 + hi], [("ps", (b + i) * 2048, (b + i + 1) * 2048) for i in range(nb)])

        def PB3(b, nb, lo, hi):
            ap = PS32[:, b * 512:(b + nb) * 512].rearrange("p (a c) -> p a c", a=nb)[:, :, lo:hi]
            return V(ap, [("ps", (b + i) * 2048, (b + i + 1) * 2048) for i in range(nb)])

        def PBh(b, lo, hi):
            return V(PS16[:, b * 1024 + lo:b * 1024 + hi], [("ps", b * 2048, (b + 1) * 2048)])

        def _ap(x):
            return x.ap if isinstance(x, V) else x

        def _tk(*xs):
            t = []
            for x in xs:
                if isinstance(x, V):
                    t += x.tok
            return t

        def ACT(out, in_, func, bias=None, scale=1.0, accum=None, eng="act"):
            kw = {"scale": _ap(scale)}
            if bias is not None:
                kw["bias"] = _ap(bias)
            if accum is not None:
                kw["accum_out"] = accum.ap
            o, i = out.ap, in_.ap
            S.add(eng, lambda e: e.activation(out=o, in_=i, func=func, **kw),
                  _tk(in_, bias, scale), _tk(out, accum))

        def TT(out, in0, in1, op, eng="dve"):
            o, a, b = out.ap, in0.ap, in1.ap
            S.add(eng, lambda e: e.tensor_tensor(out=o, in0=a, in1=b, op=op), _tk(in0, in1), _tk(out))

        def TS(out, in0, s1, op0, s2=None, op1=None, eng="dve"):
            o, a = out.ap, in0.ap
            a1, a2 = _ap(s1), _ap(s2)
            if op1 is None:
                S.add(eng, lambda e: e.tensor_scalar(out=o, in0=a, scalar1=a1, scalar2=None, op0=op0), _tk(in0, s1), _tk(out))
            else:
                S.add(eng, lambda e: e.tensor_scalar(out=o, in0=a, scalar1=a1, scalar2=a2, op0=op0, op1=op1), _tk(in0, s1, s2), _tk(out))

        def STT(out, in0, scalar, in1, op0, op1, eng="dve"):
            o, a, b, sc = out.ap, in0.ap, in1.ap, _ap(scalar)
            S.add(eng, lambda e: e.scalar_tensor_tensor(out=o, in0=a, scalar=sc, in1=b, op0=op0, op1=op1),
                  _tk(in0, in1, scalar), _tk(out))

        def COPY(out, in_, eng="dve"):
            o, a = out.ap, in_.ap
            S.add(eng, lambda e: e.tensor_copy(out=o, in_=a), _tk(in_), _tk(out))

        def RECIP(out, in_):
            o, a = out.ap, in_.ap
            S.add("dve", lambda e: e.reciprocal(out=o, in_=a), _tk(in_), _tk(out))

        def MEMSET(out, val, eng="dve"):
            o = out.ap
            S.add(eng, lambda e: e.memset(o, val), [], _tk(out))

        def REDMAXNEG(out, in_):
            o, a = out.ap, in_.ap
            S.add("dve", lambda e: e.tensor_reduce(out=o, in_=a, axis=AX.X, op=ALU.max, negate=True), _tk(in_), _tk(out))

        def MM(out, terms):
            o = out.ap
            tl = [(l.ap, r.ap) for (l, r) in terms]
            rd = []
            for (l, r) in terms:
                rd += l.tok + r.tok

            def f(e):
                n = len(tl)
                for i, (l, r) in enumerate(tl):
                    ins = e.matmul(o, l, r, start=(i == 0), stop=(i == n - 1))
                return ins
            S.add("pe", f, rd, _tk(out))

        def MMG(groups):
            rd, wr, gl = [], [], []
            for g_ in groups:
                out, terms = g_[0], g_[1]
                st_f = g_[2] if len(g_) > 2 else True
                sp_f = g_[3] if len(g_) > 3 else True
                wr += out.tok
                tl = []
                for (l, r) in terms:
                    rd += l.tok + r.tok
                    tl.append((l.ap, r.ap))
                gl.append((out.ap, tl, st_f, sp_f))

            def f(e):
                for (o, tl, st_f, sp_f) in gl:
                    n = len(tl)
                    for i, (l, r) in enumerate(tl):
                        ins = e.matmul(o, l, r, start=(i == 0 and st_f), stop=(i == n - 1 and sp_f))
                return ins
            S.add("pe", f, rd, wr)

        def TRS(pairs, ident):
            rd, wr, pl = list(ident.tok), [], []
            for (o, i) in pairs:
                rd += i.tok
                wr += o.tok
                pl.append((o.ap, i.ap))
            ia = ident.ap

            def f(e):
                for (o, i) in pl:
                    ins = e.transpose(o, i, ia)
                return ins
            S.add("pe", f, rd, wr)

        def DMA(out_ap, in_ap, rd, wr, eng="sp"):
            return S.add(eng, lambda e: e.dma_start(out=out_ap, in_=in_ap), rd, wr, dma=True)

        XT = AR.alloc(8, NT, F32)
        CB = AR.alloc(1, CB_N, BF16)
        CB2 = AR.alloc(1, 1024, BF16)
        CF = AR.alloc(1, CF_N, F32)
        SPAR = AR.alloc(DEPTH, SP_N, F32)
        CVEC = AR.alloc(1, 16, F32)
        SCB = AR.alloc(8, 2, BF16)
        MOD = AR.alloc(DEPTH, 96, F32)
        DER = AR.alloc(2, 48, F32)
        LG = AR.alloc(1, 16, F32)
        DD = AR.alloc(4, 128, F32)
        GQ = AR.alloc(4, 128, F32)
        VD = AR.alloc(1, 8, F32)
        HALO = AR.alloc(8, 1, BF16)
        SMALL = AR.alloc(1, 64, F32)
        ident = CB.s(0, CB_ID, CB_ID + 128)
        ones = CB.s(0, CB_ONE, CB_ONE + 128)
        bd64 = CB.s(0, CB_BD, CB_BD + 128)
        cs2 = CB.s(0, CB_CS2, CB_CS2 + 256)
        epsc = CF.s(0, CF_EPS, CF_EPS + 1)

        def load_full(tile, src):
            v = tile.all()
            DMA(v.ap, src, [], v.tok)

        for kc in range(8):
            v = XT.s(kc)
            DMA(v.ap, xin[:, kc, :], [], v.tok)
        DMA(CB.all().ap, cb_d, [], CB.all().tok)
        DMA(CB2.all().ap, cb2_d, [], CB2.all().tok)
        DMA(CF.all().ap, cf_d, [], CF.all().tok)
        DMA(SPAR.all().ap, spar_d, [], SPAR.all().tok)
        DMA(CVEC.all().ap, cvec, [], CVEC.all().tok)
        for which in range(2):
            o = V(SCB.ap3[:, :, which], SCB.all().tok)
            ACT(o, CVEC.s(0, which * 8, which * 8 + 8), AF.Silu)

        PH = AR.mark()

        wm_bufs = [AR.alloc(8, 1536, BF16) for _ in range(2)]
        it = 0
        for l in range(1):
            for q in range(4):
                wm = wm_bufs[it % 2]
                it += 1
                DMA(wm.all().ap, wmod_d[l, :, :, q * 1536:(q + 1) * 1536], [], wm.all().tok, eng="pool")
                b = bank()
                groups = []
                for ch in range(12):
                    terms = [(wm.s(kc, ch * 128, ch * 128 + 128), SCB.s(kc)) for kc in range(8)]
                    groups.append((PB(b, ch * 2, ch * 2 + 2), terms))
                MMG(groups)
                o = V(MOD.ap3[:, l, q * 24:(q + 1) * 24].rearrange("p (c w) -> p c w", w=2), MOD.s(l, q * 24, q * 24 + 24).tok)
                i0 = V(PS32[:, b * 512:b * 512 + 24].rearrange("p (c w) -> p c w", w=2), PB(b, 0, 24).tok)
                bm = SPAR.s(l, SP_BMOD + q * 12, SP_BMOD + q * 12 + 12)
                i1 = V(bm.ap.unsqueeze(2).to_broadcast([128, 12, 2]), bm.tok)
                TT(o, i0, i1, ALU.add)
        AR.release(PH)

        def rmsnorm_to(dst_fn, t0, n, which, kgs, ksh, sq, rs, t1s):
            ACT(sq.s((0, 4), 0, n), XT.s((0, 4), t0, t0 + n), AF.Square)
            TT(sq.s((4, 8), 0, n), XT.s((4, 8), t0, t0 + n), XT.s((4, 8), t0, t0 + n), ALU.mult)
            b = bank()
            MM(PB(b, 0, n), [(ones, sq.s(kc, 0, n)) for kc in range(8)])
            ACT(rs.s(0, 0, n), PB(b, 0, n), AF.Sqrt, bias=epsc, scale=1.0 / D)
            RECIP(rs.s(0, 0, n), rs.s(0, 0, n))
            for kc in range(8):
                t1 = t1s[kc % 2]
                STT(t1.s(0, 0, n), XT.s(kc, t0, t0 + n), DER.s(which, kgs * 8 + kc, kgs * 8 + kc + 1), rs.s(0, 0, n), ALU.mult, ALU.mult)
                ACT(dst_fn(kc), t1.s(0, 0, n), AF.Identity, bias=DER.s(which, ksh * 8 + kc, ksh * 8 + kc + 1))

        def resid_update(ob, t0, n, which, kgg, sq, rs):
            ACT(sq.s((0, 4), 0, n), ob.s((0, 4), 0, n), AF.Square)
            TT(sq.s((4, 8), 0, n), ob.s((4, 8), 0, n), ob.s((4, 8), 0, n), ALU.mult)
            b = bank()
            MM(PB(b, 0, n), [(ones, sq.s(kc, 0, n)) for kc in range(8)])
            ACT(rs.s(0, 0, n), PB(b, 0, n), AF.Sqrt, bias=epsc, scale=1.0 / D)
            RECIP(rs.s(0, 0, n), rs.s(0, 0, n))
            for kc in range(8):
                TT(ob.s(kc, 0, n), ob.s(kc, 0, n), rs.s(0, 0, n), ALU.mult, eng="pool" if kc % 2 else "dve")
                STT(XT.s(kc, t0, t0 + n), ob.s(kc, 0, n), DER.s(which, kgg * 8 + kc, kgg * 8 + kc + 1), XT.s(kc, t0, t0 + n), ALU.mult, ALU.add)

        try:
          for l in range(nlayers):
              chk("prologue")
              for which in range(2):
                  def modv(k):
                      ap = MOD.ap3[:, l, k * 16:(k + 1) * 16].rearrange("p (c w) -> p c w", w=2)[:, :, which]
                      return V(ap, MOD.s(l, k * 16, k * 16 + 16).tok)
                  STT(DER.s(which, 0, 8), modv(1), 1.0, SPAR.s(l, SP_GPM, SP_GPM + 8), ALU.add, ALU.mult)
                  COPY(DER.s(which, 8, 16), modv(0))
                  TT(DER.s(which, 16, 24), modv(2), SPAR.s(l, SP_GPO, SP_GPO + 8), ALU.mult)
                  STT(DER.s(which, 24, 32), modv(4), 1.0, SPAR.s(l, SP_GPF, SP_GPF + 8), ALU.add, ALU.mult)
                  COPY(DER.s(which, 32, 40), modv(3))
                  TT(DER.s(which, 40, 48), modv(5), SPAR.s(l, SP_GPOF, SP_GPOF + 8), ALU.mult)
              ACT(LG.s(0, 0, 8), SPAR.s(l, SP_RDB, SP_RDB + 8), AF.Exp)
              TS(LG.s(0, 0, 8), LG.s(0, 0, 8), -1.0, ALU.mult)
              ACT(LG.s(0, 8, 12), SPAR.s(l, SP_RDP, SP_RDP + 4), AF.Exp)
              TS(LG.s(0, 8, 12), LG.s(0, 8, 12), -1.0, ALU.mult)
              ACT(LG.s(0, 12, 16), LG.s(0, 8, 12), AF.Exp, scale=128.0)
              for h in range(4):
                  ACT(DD.s(h), CF.s(0, CF_DF, CF_DF + 128), AF.Exp, scale=LG.s(0, h, h + 1))
                  ACT(GQ.s(0), CF.s(0, CF_DB, CF_DB + 128), AF.Exp, scale=LG.s(0, 4 + h, 5 + h))
                  TT(DD.s(h), DD.s(h), GQ.s(0), ALU.add)
              for hp in range(2):
                  ACT(GQ.s(hp * 2 + 0), CF.s(0, CF_IL1, CF_IL1 + 128), AF.Exp, scale=LG.s(0, 8 + hp * 2, 9 + hp * 2))
                  ACT(GQ.s(hp * 2 + 1), CF.s(0, CF_ILB, CF_ILB + 128), AF.Exp, scale=LG.s(0, 9 + hp * 2, 10 + hp * 2))
              for dr in range(2):
                  for h in range(4):
                      ACT(VD.s(0, dr * 4 + h, dr * 4 + h + 1), CF.s(0, CF_JL + dr, CF_JL + dr + 1), AF.Exp, scale=LG.s(0, dr * 4 + h, dr * 4 + h + 1))

              M0 = AR.mark()
              HX = AR.alloc(8, NT, BF16)
              WB = [AR.alloc(8, 1024, BF16) for _ in range(2)]
              wb_ctr = [0]

              def load_win(c0, ncols):
                  w = WB[wb_ctr[0] % 2]
                  wb_ctr[0] += 1
                  v = w.s(None, 0, ncols)
                  DMA(v.ap, win_d[l, :, :, c0:c0 + ncols], [], v.tok, eng="pool")
                  return w

              G0 = AR.mark()
              sq = AR.alloc(8, 512, BF16)
              rs = AR.alloc(1, 512, F32)
              t1s = [AR.alloc(1, 512, F32) for _ in range(2)]
              wA = load_win(0, 768)

              def norm_blk(bi_):
                  t0_, n_ = BLOCKS[bi_]
                  rmsnorm_to(lambda kc: HX.s(kc, t0_, t0_ + n_), t0_, n_, 1 if bi_ == 0 else 0, 0, 1, sq, rs, t1s)
              norm_blk(0)
              norm_blk(1)

              def proj(w, wc0, t0, n):
                  b = bank()
                  MM(PB(b, 0, n), [(w.s(kc, wc0, wc0 + 128), HX.s(kc, t0, t0 + n)) for kc in range(8)])
                  return b

              chk("norm")
              wB0 = load_win(768, 768)
              Tt = AR.alloc(1, NT, F32)
              Bt = AR.alloc(1, NT, F32)
              acc = AR.alloc(1, NT, F32)
              cgs = [AR.alloc(1, 512, F32) for _ in range(2)]
              ya = AR.alloc(1, NT, BF16)
              for j in range(2):
                  for bi, (t0, n) in enumerate(BLOCKS):
                      if j == 0 and bi + 2 < len(BLOCKS):
                          norm_blk(bi + 2)
                      bu = proj(wA, j * 384 + 0, t0, n)
                      bb = proj(wA, j * 384 + 128, t0, n)
                      bc = proj(wA, j * 384 + 256, t0, n)
                      cg = cgs[bi % 2]
                      ACT(cg.s(0, 0, n), PB(bc, 0, n), AF.Copy)
                      TT(Tt.s(0, t0, t0 + n), PB(bu, 0, n), cg.s(0, 0, n), ALU.mult)
                      ACT(Bt.s(0, t0, t0 + n), PB(bb, 0, n), AF.Copy)
                  cw = lambda k: SPAR.s(l, SP_CW + j * 3 + k, SP_CW + j * 3 + k + 1)
                  for (s0, s1) in ((0, NCTX), (NCTX, NT)):
                      TS(acc.s(0, s0, s1), Tt.s(0, s0, s1), cw(1), ALU.mult)
                      STT(acc.s(0, s0 + 1, s1), Tt.s(0, s0, s1 - 1), cw(0), acc.s(0, s0 + 1, s1), ALU.mult, ALU.add)
                      STT(acc.s(0, s0, s1 - 1), Tt.s(0, s0 + 1, s1), cw(2), acc.s(0, s0, s1 - 1), ALU.mult, ALU.add)
                      TT(ya.s(0, s0, s1), acc.s(0, s0, s1), Bt.s(0, s0, s1), ALU.mult)
                  DMA(yd[j], ya.all().ap, ya.all().tok, ydtok(j, 0, NT))
              AR.release(G0)

              chk("A")
              wnext = None
              for hp in range(2):
                  w = wB0 if hp == 0 else wnext
                  wnext = load_win(1536, 768) if hp == 0 else load_win(2304, 256)
                  QT = AR.alloc(1, NT, BF16)
                  KT = AR.alloc(1, NT, BF16)
                  GS = AR.alloc(1, NT, BF16)
                  KR = AR.alloc(18, 128, BF16)
                  VR = AR.alloc(18, 128, BF16)
                  STB = AR.alloc(36, 64, BF16)
                  ST32 = AR.alloc(2, 64, F32)
                  mB = AR.mark()
                  VfT = AR.alloc(1, NT, BF16)
                  ropet = [AR.alloc(2, 512, F32) for _ in range(1)]
                  ta = [AR.alloc(1, 512, F32) for _ in range(2)]
                  tb = [AR.alloc(1, 512, F32) for _ in range(2)]
                  for bi, (t0, n) in enumerate(BLOCKS):
                      rt = ropet[0]
                      DMA(rt.s(None, 0, n).ap, rope_d[:, :, t0:t0 + n], [], rt.s(None, 0, n).tok)
                      for qi, (dst, sc) in enumerate(((QT, 1.0), (KT, 0.125))):
                          b0 = proj(w, qi * 128, t0, n)
                          b1 = proj(w, (4 + qi) * 128, t0, n)
                          a_, b_ = ta[qi], tb[qi]
                          TT(a_.s(0, 0, n), PB(b0, 0, n), rt.s(0, 0, n), ALU.mult)
                          TT(b_.s(0, 0, n), PB(b1, 0, n), rt.s(1, 0, n), ALU.mult)
                          TT(a_.s(0, 0, n), a_.s(0, 0, n), b_.s(0, 0, n), ALU.add, eng="pool")
                          ACT(dst.s(0, t0, t0 + n), a_.s(0, 0, n), AF.Identity, scale=sc)
                      bv = proj(w, 2 * 128, t0, n)
                      ACT(VfT.s(0, t0, t0 + n), PB(bv, 0, n), AF.Copy)
                      bg = proj(w, 3 * 128, t0, n)
                      ACT(GS.s(0, t0, t0 + n), PB(bg, 0, n), AF.Silu)
                  chk("B1")
                  for src, dst in ((KT, KR), (VfT, VR)):
                      for c0 in range(0, 18, 4):
                          cn = min(4, 18 - c0)
                          b = bank()
                          TRS([(PBh(b, i * 128, i * 128 + 128), src.s(0, (c0 + i) * 128, (c0 + i + 1) * 128)) for i in range(cn)], ident)
                          o = V(dst.ap3[:, c0:c0 + cn, :], dst.s((c0, c0 + cn)).tok)
                          i_ = V(PS16[:, b * 1024:b * 1024 + cn * 128].rearrange("p (a c) -> p a c", a=cn), PBh(b, 0, 1).tok)
                          ACT(o, i_, AF.Copy)
                  chk("B2")
                  AR.release(mB)
                  vdec = [AR.alloc(1, 128, BF16) for _ in range(4)]
                  sm = [AR.alloc(2, 128, BF16) for _ in range(3)]
                  qdec = [AR.alloc(2, 128, BF16) for _ in range(3)]
                  yr = [AR.alloc(1, 512, F32) for _ in range(2)]
                  yb16 = [AR.alloc(1, 512, BF16) for _ in range(2)]
                  dsq = [AR.alloc(1, 512, BF16) for _ in range(2)]
                  rsq = [AR.alloc(1, 512, F32) for _ in range(2)]
                  yo = AR.alloc(1, NT, BF16)
                  qzb = [AR.alloc(1, 128, BF16) for _ in range(3)]
                  for qz_ in qzb:
                      MEMSET(qz_.all(), 0.0, eng="pool")
                  for dr in range(2):
                      order = list(range(18)) if dr == 0 else [1, 0] + list(range(17, 1, -1))
                      st = ST32.s(dr)
                      MEMSET(st, 0.0)
                      for ci, c in enumerate(order):
                          COPY(STB.s(dr * 18 + c), st, eng="pool")
                          if ci == 17:
                              break
                          vd = vdec[(dr * 18 + ci) % 4]
                          vdv = VD.s(0, dr * 4 + hp * 2, dr * 4 + hp * 2 + 2)
                          TT(V(vd.flat.rearrange("p (h e) -> p h e", h=2), vd.all().tok),
                             V(VR.ap3[:, c, :].rearrange("p (h e) -> p h e", h=2), VR.s(c).tok),
                             V(vdv.ap.unsqueeze(2).to_broadcast([128, 2, 64]), vdv.tok), ALU.mult)
                          b = bank()
                          MM(PB(b, 0, 128), [(KR.s(c), vd.all())])
                          for hh in range(2):
                              STT(st.p(hh * 64, hh * 64 + 64), st.p(hh * 64, hh * 64 + 64), LG.s(0, 12 + hp * 2 + dr, 13 + hp * 2 + dr).p(hh * 64, hh * 64 + 64),
                                  PB(b, hh * 64, hh * 64 + 64).p(hh * 64, hh * 64 + 64), ALU.mult, ALU.add)
                  chk("B3")
                  def r1(c):
                      cs, ce = c * 128, (c + 1) * 128
                      qz = qzb[c % 3]
                      COPY(qz.all().p(0, 64), QT.s(0, cs, ce).p(0, 64), eng="pool")
                      qd = qdec[c % 3]
                      for dr in range(2):
                          TT(qd.s(dr), QT.s(0, cs, ce), GQ.s(hp * 2 + dr), ALU.mult, eng="pool")
                      bs = c % 3
                      MMG([(PB(bs, 0, 128), [(KT.s(0, cs, ce), qz.all())]),
                           (PB(bs, 128, 256), [(KT.s(0, cs, ce).p(64, 128), QT.s(0, cs, ce).p(64, 128))])])
                      smt = sm[c % 3]
                      ddv = DD.s((hp * 2, hp * 2 + 2))
                      TT(smt.all(), V(PS32[:, bs * 512:bs * 512 + 256].rearrange("p (h i) -> p h i", h=2), PB(bs, 0, 256).tok), ddv, ALU.mult)

                  def r2(c):
                      blk0 = (c // 4) * 4
                      ci = c - blk0
                      par = (blk0 // 4) % 2
                      yrt = yr[par]
                      smt = sm[c % 3]
                      qd = qdec[c % 3]
                      by = 3 + c % 3
                      groups = []
                      for hh in range(2):
                          ps_, pe_ = hh * 64, hh * 64 + 64
                          terms = [(V(VR.ap3[:, c, ps_:pe_], VR.s(c).tok), smt.s(hh)),
                                   (STB.s(0 * 18 + c).p(ps_, pe_), qd.s(0).p(ps_, pe_)),
                                   (STB.s(1 * 18 + c).p(ps_, pe_), qd.s(1).p(ps_, pe_))]
                          groups.append((PB(by, 0, 128).p(ps_, pe_), terms))
                      MMG(groups)
                      ACT(yrt.s(0, ci * 128, ci * 128 + 128), PB(by, 0, 128), AF.Copy)
                      if c == 17 or ci == 3:
                          cn = ci + 1
                          t0b = blk0 * 128
                          nb_ = cn * 128
                          y16 = yb16[par]
                          dq = dsq[par]
                          ACT(y16.s(0, 0, nb_), yrt.s(0, 0, nb_), AF.Copy)
                          MM(PB(6, 0, nb_), [(bd64, y16.s(0, 0, nb_))])
                          TT(yrt.s(0, 0, nb_), yrt.s(0, 0, nb_), PB(6, 0, nb_), ALU.subtract)
                          ACT(dq.s(0, 0, nb_), yrt.s(0, 0, nb_), AF.Square)
                          MM(PB(7, 0, nb_), [(bd64, dq.s(0, 0, nb_))])
                          rsv = rsq[par].s(0, 0, nb_)
                          ACT(rsv, PB(7, 0, nb_), AF.Sqrt, bias=epsc, scale=1.0)
                          RECIP(rsv, rsv)
                          TT(yrt.s(0, 0, nb_), yrt.s(0, 0, nb_), rsv, ALU.mult)
                          TT(yo.s(0, t0b, t0b + nb_), yrt.s(0, 0, nb_), GS.s(0, t0b, t0b + nb_), ALU.mult, eng="pool")

                  for step in range(19):
                      if step < 18:
                          r1(step)
                      if step >= 1:
                          r2(step - 1)
                  bank_ctr[0] = 0
                  DMA(yd[2 + hp], yo.all().ap, yo.all().tok, ydtok(2 + hp, 0, NT))
                  AR.release(G0)
              wC = wnext

              chk("B")
              wD0 = load_win(2560, 384)
              PCS = AR.alloc(18, 512, BF16)
              ys_t = [AR.alloc(2, 256, BF16) for _ in range(2)]
              ysi = [0]

              def yc_store(bj_, c0):
                  yt = ys_t[ysi[0] % 2]
                  ysi[0] += 1
                  for j in range(2):
                      if j:
                          ACT(yt.s(j), PB(bj_[j], 0, 256), AF.Copy)
                      else:
                          COPY(yt.s(j), PB(bj_[j], 0, 256))
                      DMA(yd[4 + j][:, c0:c0 + 256], yt.s(j).ap, yt.s(j).tok, ydtok(4 + j, c0, c0 + 256))
              mC = AR.mark()
              PT = [AR.alloc(1, NT, BF16) for _ in range(2)]
              for j in range(2):
                  for (t0, n) in BLOCKS:
                      b = proj(wC, j * 128, t0, n)
                      ACT(PT[j].s(0, t0, t0 + n), PB(b, 0, n), AF.Copy)
              for tc in range(18):
                  b = bank()
                  MMG([(PB(b, j * 256, j * 256 + 256), [(PT[j].s(0, tc * 128, tc * 128 + 128), cs2)]) for j in range(2)])
                  if tc % 2:
                      ACT(PCS.s(tc), PB(b, 0, 512), AF.Copy)
                  else:
                      COPY(PCS.s(tc), PB(b, 0, 512))
              AR.release(mC)
              dbuf = [AR.alloc(16, 256, BF16) for _ in range(3)]
              bj = [bank(), bank()]
              groups = []
              for j in range(2):
                  terms = []
                  for ncx in range(2):
                      terms.append((PCS.s(ncx, j * 256, j * 256 + 128), CB2.s(0, ncx * 256, ncx * 256 + 256)))
                      terms.append((PCS.s(ncx, j * 256 + 128, j * 256 + 256), CB2.s(0, 512 + ncx * 256, 512 + ncx * 256 + 256)))
                  groups.append((PB(bj[j], 0, 256), terms))
              MMG(groups)
              yc_store(bj, 0)
              di = 0
              for nb8 in range(8):
                  dc_ = dbuf[di % 3]
                  di += 1
                  ds_ = dbuf[di % 3]
                  di += 1
                  DMA(dc_.all().ap, dftc_d[nb8], [], dc_.all().tok)
                  DMA(ds_.all().ap, dfts_d[nb8], [], ds_.all().tok)
                  bj = [bank(), bank()]
                  MMG([(PB(bj[j], 0, 256), [(PCS.s(2 + ncx, j * 256, j * 256 + 128), dc_.s(ncx)) for ncx in range(16)], True, False) for j in range(2)])
                  MMG([(PB(bj[j], 0, 256), [(PCS.s(2 + ncx, j * 256 + 128, j * 256 + 256), ds_.s(ncx)) for ncx in range(16)], False, True) for j in range(2)])
                  yc_store(bj, 256 + nb8 * 256)
              AR.release(G0)

              chk("C")
              for hp in range(2):
                  w = wD0 if hp == 0 else wnext
                  if hp == 0:
                      wnext = load_win(2944, 384)
                  QT = AR.alloc(1, NT, BF16)
                  KT = AR.alloc(1, NT, BF16)
                  VfT = AR.alloc(1, NT, BF16)
                  VR = AR.alloc(18, 128, BF16)
                  BM = AR.alloc(10, 640, BF16)
                  EX = [AR.alloc(1, 896, BF16) for _ in range(3)]
                  PTs = [AR.alloc(1, 896, BF16) for _ in range(3)]
                  osb = [AR.alloc(1, 128, BF16) for _ in range(3)]
                  st_ = [AR.alloc(1, 8, F32) for _ in range(3)]
                  yo = AR.alloc(1, NT, BF16)
                  qzb = [AR.alloc(1, 128, BF16) for _ in range(3)]
                  for qz_ in qzb:
                      MEMSET(qz_.all(), 0.0, eng="pool")
                  for var in range(5):
                      v = BM.s((var * 2, var * 2 + 2))
                      DMA(v.ap, natb_d[l, var, hp * 2:hp * 2 + 2].rearrange("h p k -> p h k"), [], v.tok, eng="pool")
                  for (t0, n) in BLOCKS:
                      b0 = proj(w, 0, t0, n)
                      ACT(QT.s(0, t0, t0 + n), PB(b0, 0, n), AF.Identity, scale=0.125)
                      b1 = proj(w, 128, t0, n)
                      ACT(KT.s(0, t0, t0 + n), PB(b1, 0, n), AF.Copy)
                      b2 = proj(w, 256, t0, n)
                      ACT(VfT.s(0, t0, t0 + n), PB(b2, 0, n), AF.Copy)
                  for c0 in range(0, 18, 4):
                      cn = min(4, 18 - c0)
                      b = bank()
                      TRS([(PBh(b, i * 128, i * 128 + 128), VfT.s(0, (c0 + i) * 128, (c0 + i + 1) * 128)) for i in range(cn)], ident)
                      o = V(VR.ap3[:, c0:c0 + cn, :], VR.s((c0, c0 + cn)).tok)
                      i_ = V(PS16[:, b * 1024:b * 1024 + cn * 128].rearrange("p (a c) -> p a c", a=cn), PBh(b, 0, 1).tok)
                      ACT(o, i_, AF.Copy)
                  iters = [(qt, hh) for qt in range(-2 if l < DEPTH - 1 else 0, 16) for hh in range(2)]

                  def ngeo(qt):
                      if qt < 0:
                          return (qt + 2) * 128, 256, [0, 1], 0, 0
                      kr0e, var = nat_geom(qt)
                      kc0 = 2 + kr0e // 2
                      return NCTX + qt * 128, 896, [kc0 + i for i in range(5)] + [0, 1], kc0, var

                  def s1(i):
                      qt, hh = iters[i]
                      qs_, nk, kchunks, kc0, var = ngeo(qt)
                      qz = qzb[(qt + 2) % 3]
                      if hh == 0:
                          COPY(qz.all().p(0, 64), QT.s(0, qs_, qs_ + 128).p(0, 64), eng="pool")
                          qv = qz.all()
                          kv_ = lambda a_, b_: KT.s(0, a_, b_)
                      else:
                          qv = QT.s(0, qs_, qs_ + 128).p(64, 128)
                          kv_ = lambda a_, b_: KT.s(0, a_, b_).p(64, 128)
                      bsc = 0 if i % 2 == 0 else 2
                      if qt < 0:
                          MM(PB(bsc, 0, 256), [(qv, kv_(0, 256))])
                      else:
                          ks_ = kc0 * 128
                          MMG([
                              (PB(bsc, 0, 512), [(qv, kv_(ks_, ks_ + 512)), (ident, BM.s(var * 2 + hh, 0, 512))]),
                              (PB(bsc + 1, 0, 128), [(qv, kv_(ks_ + 512, ks_ + 640)), (ident, BM.s(var * 2 + hh, 512, 640))]),
                              (PB(bsc + 1, 128, 384), [(qv, kv_(0, 256))]),
                          ])
                      sv = st_[i % 3]
                      ex = EX[i % 3]
                      scv = PB(bsc, 0, nk, nb=2)
                      REDMAXNEG(sv.s(0, 0, 1), scv)
                      ACT(ex.s(0, 0, nk), scv, AF.Exp, bias=sv.s(0, 0, 1), scale=1.0, accum=sv.s(0, 1, 2))
                      RECIP(sv.s(0, 2, 3), sv.s(0, 1, 2))

                  def s2(i):
                      qt, hh = iters[i]
                      qs_, nk, kchunks, kc0, var = ngeo(qt)
                      ex = EX[i % 3]
                      pt = PTs[i % 3]
                      bt = 4 + i % 2
                      nch = nk // 128
                      TRS([(PBh(bt, k * 128, k * 128 + 128), ex.s(0, k * 128, k * 128 + 128)) for k in range(nch)], ident)
                      if i % 3 == 0:
                          ACT(pt.s(0, 0, nk), PBh(bt, 0, nk), AF.Copy)
                      else:
                          COPY(pt.s(0, 0, nk), PBh(bt, 0, nk))

                  def s3(i):
                      qt, hh = iters[i]
                      qs_, nk, kchunks, kc0, var = ngeo(qt)
                      pt = PTs[i % 3]
                      sv = st_[i % 3]
                      ps_, pe_ = hh * 64, hh * 64 + 64
                      nch = nk // 128
                      oc = ((qt + 2) % 2) * 128 + hh * 64
                      MM(PB(6, oc, oc + 64), [(pt.s(0, k * 128, k * 128 + 128), V(VR.ap3[:, kchunks[k], ps_:pe_], VR.s(kchunks[k]).tok)) for k in range(nch)])
                      ot = osb[(qt + 2) % 3]
                      ACT(ot.s(0, hh * 64, hh * 64 + 64), PB(6, oc, oc + 64), AF.Identity, scale=sv.s(0, 2, 3))
                      if hh == 1:
                          TRS([(PBh(7, ((qt + 2) % 2) * 128, ((qt + 2) % 2) * 128 + 128), ot.all())], ident)
                          COPY(yo.s(0, qs_, qs_ + 128), PBh(7, ((qt + 2) % 2) * 128, ((qt + 2) % 2) * 128 + 128))

                  NI = len(iters)
                  for step in range(NI + 2):
                      if step < NI:
                          s1(step)
                      if 0 <= step - 1 < NI:
                          s2(step - 1)
                      if 0 <= step - 2 < NI:
                          s3(step - 2)
                  bank_ctr[0] = 0
                  DMA(yd[6 + hp], yo.all().ap, yo.all().tok, ydtok(6 + hp, 0, NT))
                  AR.release(G0)

              chk("D")
              wo = WB[wb_ctr[0] % 2]
              wb_ctr[0] += 1
              DMA(wo.all().ap, wout_d[l], [], wo.all().tok, eng="pool")
              yb = [AR.alloc(8, 512, BF16) for _ in range(2)]
              ob = AR.alloc(8, 512, F32)
              sq = AR.alloc(8, 512, BF16)
              rs = AR.alloc(1, 512, F32)
              last_layer = (l == DEPTH - 1)
              for bi, (t0, n) in enumerate(BLOCKS):
                  if last_layer and bi == 0:
                      continue
                  which = 1 if bi == 0 else 0
                  ybt = yb[bi % 2]
                  v = ybt.s(None, 0, n)
                  rd = []
                  for c in range(8):
                      rd += ydtok(c, t0, t0 + n)
                  DMA(v.ap, yd[:, :, t0:t0 + n].rearrange("c p t -> p c t"), rd, v.tok)
                  for dc in range(8):
                      b = bank()
                      MM(PB(b, 0, n), [(wo.s(m, dc * 128, dc * 128 + 128), ybt.s(m, 0, n)) for m in range(8)])
                      if dc % 2:
                          ACT(ob.s(dc, 0, n), PB(b, 0, n), AF.Copy)
                      else:
                          COPY(ob.s(dc, 0, n), PB(b, 0, n))
                  resid_update(ob, t0, n, which, 2, sq, rs)
              AR.release(M0)
              if dbg == ("mix", l):
                  break

              F0 = AR.mark()
              HX2 = AR.alloc(8, 514, BF16)
              G = AR.alloc(22, 512, BF16)
              WUP = [AR.alloc(8, 256, BF16) for _ in range(4)]
              WDN = AR.alloc(22, 1024, BF16)
              DMA(WDN.all().ap, wdn_d[l], [], WDN.all().tok, eng="pool")
              WMB = [AR.alloc(8, 256, BF16) for _ in range(2)]
              do_mod = (l + 1 < nlayers)

              def mod_dma(p):
                  wm = WMB[p % 2]
                  DMA(wm.all().ap, wmod_d[l + 1, :, :, p * 256:(p + 1) * 256], [], wm.all().tok, eng="pool")

              def mod_compute(p):
                  wm = WMB[p % 2]
                  b = bank()
                  MMG([(PB(b, ch * 2, ch * 2 + 2), [(wm.s(kc, ch * 128, ch * 128 + 128), SCB.s(kc)) for kc in range(8)]) for ch in range(2)])
                  o = V(MOD.ap3[:, l + 1, p * 4:p * 4 + 4].rearrange("p (c w) -> p c w", w=2), MOD.s(l + 1, p * 4, p * 4 + 4).tok)
                  i0 = V(PS32[:, b * 512:b * 512 + 4].rearrange("p (c w) -> p c w", w=2), PB(b, 0, 4).tok)
                  bm = SPAR.s(l + 1, SP_BMOD + p * 2, SP_BMOD + p * 2 + 2)
                  i1 = V(bm.ap.unsqueeze(2).to_broadcast([128, 2, 2]), bm.tok)
                  TT(o, i0, i1, ALU.add)
              if do_mod:
                  mod_dma(0)
                  mod_dma(1)
              FS = AR.mark()
              ffn_blocks = [bi_ for bi_ in range(len(BLOCKS)) if not (last_layer and bi_ == 0)]
              wu_seq = [(bi_, j_) for bi_ in ffn_blocks for j_ in range(22)]
              wu_issued = [0]

              def wu_prefetch(upto):
                  while wu_issued[0] < min(upto, len(wu_seq)):
                      k_ = wu_issued[0]
                      wt_ = WUP[k_ % 4]
                      DMA(wt_.all().ap, wup_d[l, wu_seq[k_][1]], [], wt_.all().tok, eng="pool")
                      wu_issued[0] += 1
              wu_prefetch(3)
              for fi, bi in enumerate(ffn_blocks):
                  t0, n = BLOCKS[bi]
                  which = 1 if bi == 0 else 0
                  seq_start = t0 in (0, NCTX)
                  seq_end = (t0 + n) in (NCTX, NT)
                  sq = AR.alloc(8, 514, BF16)
                  rs = AR.alloc(1, 514, F32)
                  t1s = [AR.alloc(1, 514, F32) for _ in range(2)]
                  if seq_start:
                      MEMSET(HX2.s(None, 0, 1), 0.0)
                  else:
                      COPY(HX2.s(None, 0, 1), HALO.s(None, 0, 1))
                  rmsnorm_to(lambda kc: HX2.s(kc, 1, 1 + n), t0, n, which, 3, 4, sq, rs, t1s)
                  if seq_end:
                      MEMSET(HX2.s(None, n + 1, n + 2), 0.0)
                  else:
                      rmsnorm_to(lambda kc: HX2.s(kc, n + 1, n + 2), t0 + n, 1, which, 3, 4, sq, rs, t1s)
                  COPY(HALO.s(None, 0, 1), HX2.s(None, n, n + 1))
                  AR.release(FS)
                  accA = [AR.alloc(2, 256, F32) for _ in range(2)]
                  accB = [AR.alloc(2, 256, F32) for _ in range(2)]
                  sa = [AR.alloc(2, 256, F32) for _ in range(2)]
                  ncb = 2 if n == 512 else 1
                  cw_ = n // ncb
                  for j in range(22):
                      wu = WUP[(fi * 22 + j) % 4]
                      wu_prefetch(fi * 22 + j + 4)
                      g_ = fi * 22 + j
                      if do_mod and g_ % 4 == 0 and g_ // 4 < 24:
                          mod_compute(g_ // 4)
                          if g_ // 4 + 2 < 24:
                              mod_dma(g_ // 4 + 2)
                      ba = bank2()
                      bb = bank2()
                      groups = []
                      for (bk, wc0) in ((ba, 0), (bb, 128)):
                          for cbk in range(ncb):
                              c0 = cbk * cw_
                              groups.append((PB(bk + cbk, 0, cw_ + 2), [(wu.s(kc, wc0, wc0 + 128), HX2.s(kc, c0, c0 + cw_ + 2)) for kc in range(8)]))
                      MMG(groups)
                      for (bk, accs, ch) in ((ba, accA, j), (bb, accB, 22 + j)):
                          at = accs[j % 2]
                          av = V(at.ap3[:, 0:ncb, 0:cw_], at.all().tok)
                          fw = lambda k: SPAR.s(l, SP_FCW + ch * 3 + k, SP_FCW + ch * 3 + k + 1)
                          ACT(av, PB3(bk, ncb, 1, cw_ + 1), AF.Identity, scale=fw(1))
                          STT(av, PB3(bk, ncb, 0, cw_), fw(0), av, ALU.mult, ALU.add)
                          STT(av, PB3(bk, ncb, 2, cw_ + 2), fw(2), av, ALU.mult, ALU.add)
                      sat = sa[j % 2]
                      sav = V(sat.ap3[:, 0:ncb, 0:cw_], sat.all().tok)
                      aAv = V(accA[j % 2].ap3[:, 0:ncb, 0:cw_], accA[j % 2].all().tok)
                      aBv = V(accB[j % 2].ap3[:, 0:ncb, 0:cw_], accB[j % 2].all().tok)
                      ACT(sav, aAv, AF.Silu)
                      gv = V(G.ap3[:, j, 0:n].rearrange("p (a c) -> p a c", a=ncb), G.s(j, 0, n).tok)
                      TT(gv, sav, aBv, ALU.mult, eng="pool")
                  AR.release(FS)
                  ob = AR.alloc(8, 512, F32)
                  sq = HX2
                  rs = AR.alloc(1, 512, F32)
                  for dc in range(8):
                      b = bank()
                      MM(PB(b, 0, n), [(WDN.s(j, dc * 128, dc * 128 + 128), G.s(j, 0, n)) for j in range(22)])
                      if dc % 2:
                          ACT(ob.s(dc, 0, n), PB(b, 0, n), AF.Copy)
                      else:
                          COPY(ob.s(dc, 0, n), PB(b, 0, n))
                  resid_update(ob, t0, n, which, 5, sq, rs)
                  AR.release(FS)
              AR.release(F0)

        except _Stop:
            pass
        finals = []
        for kc in range(8):
            v = XT.s(kc, NCTX, NT)
            finals.append(DMA(yout[:, kc, :], v.ap, v.tok, []))
        if dbg:
            for kc in range(8):
                v = XT.s(kc)
                finals.append(DMA(xdbg[:, kc, :], v.ap, v.tok, []))
            for rec in S.recs.get("yd", []):
                if rec[2] is not None:
                    finals.append(rec[2])
        S.emit(nc, finals)
    build_program.last_stats = (S.nops, AR.peak)
    return nc


_CONST_CACHE = {}


def _consts():
    if _CONST_CACHE:
        return _CONST_CACHE
    bf = ml_dtypes.bfloat16
    cb = np.zeros((128, CB_N), np.float32)
    cb[:, CB_ID:CB_ID + 128] = np.eye(128)
    cb[:, CB_ONE:CB_ONE + 128] = 1.0
    bd = np.zeros((128, 128), np.float32)
    bd[:64, :64] = 1.0 / 64
    bd[64:, 64:] = 1.0 / 64
    cb[:, CB_BD:CB_BD + 128] = bd
    cc = np.arange(64)
    ang = 2 * np.pi * ((cc[:, None] * cc[None, :]) % 64) / 64.0
    Cc, Sc = np.cos(ang) / 8.0, np.sin(ang) / 8.0
    z = np.zeros((64, 64))
    cb[:, CB_CS2:CB_CS2 + 128] = np.block([[Cc, z], [z, Cc]])
    cb[:, CB_CS2 + 128:CB_CS2 + 256] = -np.block([[Sc, z], [z, Sc]])
    _CONST_CACHE["cb"] = cb.astype(bf)

    def dft(n):
        k = np.arange(n)
        a = 2 * np.pi * ((k[:, None] * k[None, :]) % n) / float(n)
        return np.cos(a) / np.sqrt(n), np.sin(a) / np.sqrt(n)
    c256, s256 = dft(256)
    cb2 = np.zeros((128, 1024), np.float32)
    for ncx in range(2):
        cb2[:, ncx * 256:(ncx + 1) * 256] = c256[ncx * 128:(ncx + 1) * 128, :]
        cb2[:, 512 + ncx * 256:512 + (ncx + 1) * 256] = s256[ncx * 128:(ncx + 1) * 128, :]
    _CONST_CACHE["cb2"] = cb2.astype(bf)
    c2k, s2k = dft(2048)
    _CONST_CACHE["dftc"] = np.ascontiguousarray(c2k.reshape(16, 128, 8, 256).transpose(2, 1, 0, 3)).astype(bf)
    _CONST_CACHE["dfts"] = np.ascontiguousarray(s2k.reshape(16, 128, 8, 256).transpose(2, 1, 0, 3)).astype(bf)
    cf = np.zeros((128, CF_N), np.float32)
    jj = np.arange(128)[:, None]
    ii = np.arange(128)[None, :]
    BIG = 1.0e5
    cf[:, CF_DF:CF_DF + 128] = np.where(ii >= jj, ii - jj, BIG)
    cf[:, CF_DB:CF_DB + 128] = np.where(jj >= ii, jj - ii, BIG)
    cf[:, CF_IL1:CF_IL1 + 128] = ii + 1.0
    cf[:, CF_ILB:CF_ILB + 128] = 128.0 - ii
    cf[:, CF_JL] = 127.0 - np.arange(128)
    cf[:, CF_JL + 1] = np.arange(128)
    cf[:, CF_EPS] = EPS
    _CONST_CACHE["cf"] = cf
    t = np.arange(NX)
    row = (t // 64).astype(np.float32)
    col = (t % 64).astype(np.float32)
    inv = (np.float32(10000.0) ** (-np.arange(16, dtype=np.float32) / np.float32(16))).astype(np.float32)
    angx = np.concatenate([row[:, None] * inv, col[:, None] * inv], -1).astype(np.float32)
    cosx, sinx = np.cos(angx).astype(np.float32), np.sin(angx).astype(np.float32)
    rope = np.zeros((128, 2, NT), np.float32)
    rope[:, 0, :NCTX] = 1.0
    for p in range(128):
        d = p % 64
        f = d % 32
        rope[p, 0, NCTX:] = cosx[:, f]
        rope[p, 1, NCTX:] = -sinx[:, f] if d < 32 else sinx[:, f]
    _CONST_CACHE["rope"] = rope
    return _CONST_CACHE


def _win_cols():
    cols = []
    for j in range(2):
        for part in range(3):
            cols.append(np.arange(128) + part * 256 + j * 128)
    for hp in range(2):
        base = 768
        for part in range(4):
            cols.append(base + part * 256 + hp * 128 + np.arange(128))
        for part in range(2):
            p = np.arange(128)
            cols.append(base + part * 256 + hp * 128 + (p // 64) * 64 + ((p % 64) + 32) % 64)
    for j in range(2):
        cols.append(1792 + j * 128 + np.arange(128))
    for hp in range(2):
        for part in range(3):
            cols.append(2048 + part * 256 + hp * 128 + np.arange(128))
    return np.concatenate(cols)


def _nat_tables(rpb_l):
    out = np.full((5, 4, 128, 640), -1.0e30, np.float32)
    for vi, qt in enumerate([2, 0, 1, 14, 15]):
        s0 = min(max(2 * qt - 4, 0), 24)
        kr0e = min(s0 - s0 % 2, 22)
        for rr in range(2):
            r = 2 * qt + rr
            start = min(max(r - 4, 0), 24)
            for cq in range(64):
                q = rr * 64 + cq
                qcs = min(max(cq - 8, 0), 48)
                kc = np.arange(qcs, qcs + 16)
                dcol = np.clip(kc - cq + 15, 0, 30)
                for kr in range(10):
                    keyrow = kr0e + kr
                    if not (start <= keyrow < start + 8):
                        continue
                    drow = keyrow - r + 7
                    out[vi, :, q, kr * 64 + qcs:kr * 64 + qcs + 16] = rpb_l[:, drow, dcol]
    return out


def _fm(v):
    return np.ascontiguousarray(v.reshape(-1, 128).T)


def host_prep(inp):
    f32 = np.float32
    C = _consts()
    shared = {k: C[k] for k in ("cb", "cb2", "cf", "rope", "dftc", "dfts")}
    spar = np.zeros((128, DEPTH, SP_N), f32)
    for l in range(DEPTH):
        spar[:, l, SP_GPM:SP_GPM + 8] = _fm(inp["g_pre_mix"][l])
        spar[:, l, SP_GPO:SP_GPO + 8] = _fm(inp["g_post_mix"][l])
        spar[:, l, SP_GPF:SP_GPF + 8] = _fm(inp["g_pre_ffn"][l])
        spar[:, l, SP_GPOF:SP_GPOF + 8] = _fm(inp["g_post_ffn"][l])
        spar[:, l, SP_BMOD:SP_BMOD + 48] = _fm(inp["b_mod"][l])
        cw = inp["conv_w"][l]
        for j in range(2):
            for k in range(3):
                spar[:, l, SP_CW + j * 3 + k] = cw[k, j * 128:(j + 1) * 128]
        fw = inp["ffn_conv_w"][l]
        for ch in range(44):
            for k in range(3):
                spar[:, l, SP_FCW + ch * 3 + k] = fw[k, ch * 128:(ch + 1) * 128]
        rd = inp["ret_decay"][l]
        for dr in range(2):
            for h in range(4):
                spar[:, l, SP_RDB + dr * 4 + h] = rd[dr, h]
        for hp in range(2):
            for dr in range(2):
                spar[:64, l, SP_RDP + hp * 2 + dr] = rd[dr, 2 * hp]
                spar[64:, l, SP_RDP + hp * 2 + dr] = rd[dr, 2 * hp + 1]
    shared["spar"] = spar
    kmaj = lambda w: np.ascontiguousarray(w.reshape(8, 128, -1).transpose(1, 0, 2))
    cols = _win_cols()
    shared["wmod"] = np.stack([kmaj(np.asarray(inp["w_mod"][l], f32)) for l in range(DEPTH)])
    shared["win"] = np.stack([kmaj(np.asarray(inp["w_in"][l], f32)[:, cols]) for l in range(DEPTH)])
    shared["wout"] = np.stack([kmaj(np.asarray(inp["w_out"][l], f32)) for l in range(DEPTH)])
    wup = np.zeros((DEPTH, 22, 128, 8, 256), f32)
    for l in range(DEPTH):
        w = kmaj(np.asarray(inp["w_up"][l], f32))
        for j in range(22):
            wup[l, j, :, :, 0:128] = w[:, :, j * 128:(j + 1) * 128]
            wup[l, j, :, :, 128:256] = w[:, :, DFF + j * 128:DFF + (j + 1) * 128]
    shared["wup"] = wup
    shared["wdn"] = np.stack([np.ascontiguousarray(np.asarray(inp["w_down"][l], f32).reshape(22, 128, 1024).transpose(1, 0, 2)) for l in range(DEPTH)])
    shared["natb"] = np.stack([_nat_tables(np.asarray(inp["nat_rpb"][l], f32)) for l in range(DEPTH)])
    per_core = []
    x = np.asarray(inp["x"], f32)
    ctx = np.asarray(inp["ctx"], f32)
    c = np.asarray(inp["c"], f32)
    cc = _fm(np.asarray(inp["c_ctx"], f32))
    for b in range(x.shape[0]):
        cat = np.concatenate([ctx[b], x[b]], 0)
        xin = np.ascontiguousarray(cat.T.reshape(8, 128, NT).transpose(1, 0, 2))
        cvec = np.concatenate([_fm(c[b]), cc], 1).astype(f32)
        per_core.append({"xin": xin, "cvec": cvec})
    return shared, per_core


_PROG = {}


def kernel(**inputs):
    shared, per_core = host_prep(inputs)
    if "nc" not in _PROG:
        _PROG["nc"] = build_program()
    nc = _PROG["nc"]
    n = len(per_core)
    in_maps = [dict(shared, **pc) for pc in per_core]
    res = run_bass_kernel_spmd(nc, in_maps, core_ids=list(range(n)))
    out = np.zeros((n, NX, D), np.float32)
    for b in range(n):
        y = res.results[b]["yout"]
        out[b] = y.transpose(2, 1, 0).reshape(NX, D)
    return out
```
